# Optimizing a Trainium2 kernel written in Bass

```python
import math
import jax, jax.numpy as jnp
from jax import lax
import numpy as np

D_MODEL = 2048
BATCH = 2
SEQ = 4096
DEPTH = 1

N_HEADS = 8
N_KV = 2
HPG = N_HEADS // N_KV
HEAD_DIM = 128
ATTN_WIDTH = N_HEADS * HEAD_DIM
KV_WIDTH = N_KV * HEAD_DIM
CONV_WIDTH = D_MODEL - ATTN_WIDTH
MIX_WIDTH = ATTN_WIDTH + CONV_WIDTH
CONV_K = 3
N_BRANCH = 3
CMP_LEN = 32
CMP_STRIDE = 16
CMP_HIDDEN = 256
SEL_LEN = 64
SEL_TOPK = 16
WINDOW = 512
Q_BLOCK = 128
N_BUCKETS = 32
MAX_DIST = 128
D_FF = 4 * D_MODEL
EPS = 1e-6
NEG = -1e30
FORCE = 1e9
SPLIT_SIZES = [ATTN_WIDTH] + [KV_WIDTH] * 6 + [N_HEADS * N_BRANCH] + [CONV_WIDTH] * 3
IN_WIDTH = sum(SPLIT_SIZES)

kernel_name = "hymba_nsa_shortconv_sandwich_layer"


def rmsnorm(x, g):
    xf = x.astype(jnp.float32)
    y = xf * lax.rsqrt(jnp.mean(xf * xf, axis=-1, keepdims=True) + EPS)
    return (y * g.astype(jnp.float32)).astype(x.dtype)


def rel_bucket(dist):
    n = jnp.maximum(dist, 0)
    max_exact = N_BUCKETS // 2
    nf = jnp.maximum(n, 1).astype(jnp.float32)
    large = max_exact + (jnp.log(nf / max_exact) / math.log(MAX_DIST / max_exact)
                         * (N_BUCKETS - max_exact)).astype(jnp.int32)
    large = jnp.minimum(large, N_BUCKETS - 1)
    return jnp.where(n < max_exact, n, large)


def masked_softmax(s, valid):
    s = jnp.where(valid, s.astype(jnp.float32), NEG)
    p = jax.nn.softmax(s, axis=-1)
    return jnp.where(valid, p, 0.0)


def compress(raw, pe, w1, w2):
    t = raw.shape[1]
    n_cmp = (t - CMP_LEN) // CMP_STRIDE + 1
    idx = np.arange(n_cmp)[:, None] * CMP_STRIDE + np.arange(CMP_LEN)[None, :]
    blk = raw[:, idx] + pe[None, None, :, None, :]
    hid = jax.nn.silu(jnp.einsum('bclgd,ldh->bcgh', blk, w1))
    return jnp.einsum('bcgh,he->bgce', hid, w2)


def nsa(q, kc, vc, ks, vs, kw, vw, gates, pe, wk1, wk2, wv1, wv2, rel_bias):
    b, t = q.shape[0], q.shape[1]
    scale = HEAD_DIM ** -0.5
    pos = jnp.arange(t)
    qg = q.reshape(b, t, N_KV, HPG, HEAD_DIM).transpose(0, 2, 3, 1, 4)

    kcmp = compress(kc, pe, wk1, wk2)
    vcmp = compress(vc, pe, wv1, wv2)
    n_cmp = kcmp.shape[2]
    blk_end = jnp.arange(n_cmp) * CMP_STRIDE + CMP_LEN - 1
    dist_c = pos[:, None] - blk_end[None, :]
    valid_c = dist_c >= 0
    bias_c = rel_bias[:, rel_bucket(dist_c)].reshape(N_KV, HPG, t, n_cmp)
    s_c = jnp.einsum('bghtd,bgcd->bghtc', qg, kcmp).astype(jnp.float32) * scale + bias_c
    p_c = masked_softmax(s_c, valid_c)
    o_c = jnp.einsum('bghtc,bgcd->bghtd', p_c.astype(vcmp.dtype), vcmp)

    n_sel = t // SEL_LEN
    ci = np.arange(n_cmp)[:, None] * CMP_STRIDE
    sj = np.arange(n_sel)[None, :] * SEL_LEN
    overlap = ((ci < sj + SEL_LEN) & (ci + CMP_LEN > sj)).astype(np.float32)
    imp = jnp.einsum('bghtc,cj->bgtj', p_c, jnp.asarray(overlap))
    j = jnp.arange(n_sel)[None, :]
    cur = (pos // SEL_LEN)[:, None]
    forced = (j == 0) | (j == cur) | (j == cur - 1)
    imp = jnp.where(forced, FORCE, imp)
    imp = jnp.where(j * SEL_LEN <= pos[:, None], imp, NEG)
    k_top = min(SEL_TOPK, n_sel)
    _, sel_idx = lax.top_k(imp, k_top)

    ks_blk = ks.transpose(0, 2, 1, 3).reshape(b, N_KV, n_sel, SEL_LEN, HEAD_DIM)
    vs_blk = vs.transpose(0, 2, 1, 3).reshape(b, N_KV, n_sel, SEL_LEN, HEAD_DIM)
    kw_pad = jnp.pad(kw.transpose(0, 2, 1, 3), ((0, 0), (0, 0), (WINDOW, 0), (0, 0)))
    vw_pad = jnp.pad(vw.transpose(0, 2, 1, 3), ((0, 0), (0, 0), (WINDOW, 0), (0, 0)))

    rel_bias_g = rel_bias.reshape(N_KV, HPG, N_BUCKETS)
    bi = jnp.arange(b)[:, None, None, None]
    gi = jnp.arange(N_KV)[None, :, None, None]
    gi5 = jnp.arange(N_KV)[None, :, None, None, None]
    hi5 = jnp.arange(HPG)[None, None, :, None, None]
    kv_len = Q_BLOCK + WINDOW
    dist_w = WINDOW + jnp.arange(Q_BLOCK)[:, None] - jnp.arange(kv_len)[None, :]
    band = (dist_w >= 0) & (dist_w < WINDOW)
    bias_w = rel_bias[:, rel_bucket(dist_w)].reshape(N_KV, HPG, Q_BLOCK, kv_len)

    def block_fn(c):
        start = c * Q_BLOCK
        qb = lax.dynamic_slice_in_dim(qg, start, Q_BLOCK, axis=3)
        tq = start + jnp.arange(Q_BLOCK)
        idx = lax.dynamic_slice_in_dim(sel_idx, start, Q_BLOCK, axis=2)
        kb = ks_blk[bi, gi, idx].reshape(b, N_KV, Q_BLOCK, k_top * SEL_LEN, HEAD_DIM)
        vb = vs_blk[bi, gi, idx].reshape(b, N_KV, Q_BLOCK, k_top * SEL_LEN, HEAD_DIM)
        spos = (idx[..., None] * SEL_LEN + jnp.arange(SEL_LEN)).reshape(b, N_KV, Q_BLOCK, k_top * SEL_LEN)
        dist_s = tq[None, None, :, None] - spos
        valid_s = (dist_s >= 0)[:, :, None]
        bias_s = rel_bias_g[gi5, hi5, rel_bucket(dist_s)[:, :, None]]
        s_s = jnp.einsum('bghqd,bgqsd->bghqs', qb, kb).astype(jnp.float32) * scale + bias_s
        p_s = masked_softmax(s_s, valid_s)
        o_s = jnp.einsum('bghqs,bgqsd->bghqd', p_s.astype(vb.dtype), vb)
        kwb = lax.dynamic_slice_in_dim(kw_pad, start, kv_len, axis=2)
        vwb = lax.dynamic_slice_in_dim(vw_pad, start, kv_len, axis=2)
        kpos = start - WINDOW + jnp.arange(kv_len)
        valid_w = band & (kpos >= 0)[None, :]
        s_w = jnp.einsum('bghqd,bgkd->bghqk', qb, kwb).astype(jnp.float32) * scale + bias_w
        p_w = masked_softmax(s_w, valid_w)
        o_w = jnp.einsum('bghqk,bgkd->bghqd', p_w.astype(vwb.dtype), vwb)
        return o_s, o_w

    n_blocks = t // Q_BLOCK
    o_s, o_w = lax.map(block_fn, jnp.arange(n_blocks))

    def unblock(o):
        return o.transpose(1, 2, 3, 0, 4, 5).reshape(b, N_KV, HPG, t, HEAD_DIM)

    def to_bthd(o):
        return o.transpose(0, 3, 1, 2, 4).reshape(b, t, N_HEADS, HEAD_DIM)

    o_c, o_s, o_w = to_bthd(o_c), to_bthd(unblock(o_s)), to_bthd(unblock(o_w))
    g = gates.astype(o_c.dtype)
    o = g[..., 0:1] * o_c + g[..., 1:2] * o_s + g[..., 2:3] * o_w
    return o.reshape(b, t, ATTN_WIDTH)


def short_conv(h, bg, cg, w):
    u = cg * h
    t = u.shape[1]
    up = jnp.pad(u, ((0, 0), (CONV_K - 1, 0), (0, 0)))
    y = w[0] * up[:, 0:t]
    for k in range(1, CONV_K):
        y = y + w[k] * up[:, k:k + t]
    return bg * y


def setup_inputs(seed: int = 0) -> dict:
    key = jax.random.key(seed)
    ks = jax.random.split(key, 20)
    L = DEPTH
    nrm = lambda k, shape, s: jax.random.normal(k, shape, jnp.float32) * s
    gain = lambda k: 1.0 + 0.02 * jax.random.normal(k, (L, D_MODEL), jnp.float32)
    return {
        "x": nrm(ks[0], (BATCH, SEQ, D_MODEL), 1.0),
        "w_in": nrm(ks[1], (L, D_MODEL, IN_WIDTH), D_MODEL ** -0.5),
        "pe_cmp": nrm(ks[2], (L, CMP_LEN, HEAD_DIM), 0.02),
        "w_cmp_k1": nrm(ks[3], (L, CMP_LEN, HEAD_DIM, CMP_HIDDEN), (CMP_LEN * HEAD_DIM) ** -0.5),
        "w_cmp_k2": nrm(ks[4], (L, CMP_HIDDEN, HEAD_DIM), CMP_HIDDEN ** -0.5),
        "w_cmp_v1": nrm(ks[5], (L, CMP_LEN, HEAD_DIM, CMP_HIDDEN), (CMP_LEN * HEAD_DIM) ** -0.5),
        "w_cmp_v2": nrm(ks[6], (L, CMP_HIDDEN, HEAD_DIM), CMP_HIDDEN ** -0.5),
        "conv_w": nrm(ks[7], (L, CONV_K, CONV_WIDTH), CONV_K ** -0.5),
        "rel_bias": nrm(ks[8], (N_HEADS, N_BUCKETS), 0.5),
        "w_o": nrm(ks[9], (L, MIX_WIDTH, D_MODEL), MIX_WIDTH ** -0.5),
        "w_up": nrm(ks[10], (L, D_MODEL, D_FF), D_MODEL ** -0.5),
        "w_down": nrm(ks[11], (L, D_FF, D_MODEL), D_FF ** -0.5),
        "g_pre_mix": gain(ks[12]),
        "g_post_mix": gain(ks[13]),
        "g_pre_ffn": gain(ks[14]),
        "g_post_ffn": gain(ks[15]),
    }


def reference(x, w_in, pe_cmp, w_cmp_k1, w_cmp_k2, w_cmp_v1, w_cmp_v2, conv_w, rel_bias,
              w_o, w_up, w_down, g_pre_mix, g_post_mix, g_pre_ffn, g_post_ffn):
    b, t, _ = x.shape
    offsets = np.cumsum(SPLIT_SIZES)[:-1].tolist()
    for l in range(DEPTH):
        h = rmsnorm(x, g_pre_mix[l])
        proj = h @ w_in[l]
        q, kc, vc, ksl, vsl, kwn, vwn, gl, ch, cb, cc = jnp.split(proj, offsets, axis=-1)
        kv = lambda a: a.reshape(b, t, N_KV, HEAD_DIM)
        gates = jax.nn.sigmoid(gl.astype(jnp.float32)).reshape(b, t, N_HEADS, N_BRANCH)
        o_attn = nsa(q.reshape(b, t, N_HEADS, HEAD_DIM), kv(kc), kv(vc), kv(ksl), kv(vsl),
                     kv(kwn), kv(vwn), gates, pe_cmp[l], w_cmp_k1[l], w_cmp_k2[l],
                     w_cmp_v1[l], w_cmp_v2[l], rel_bias)
        o_conv = short_conv(ch, cb, cc, conv_w[l])
        mix = jnp.concatenate([o_attn, o_conv], axis=-1) @ w_o[l]
        x = x + rmsnorm(mix, g_post_mix[l])
        h = rmsnorm(x, g_pre_ffn[l])
        f = jnp.square(jax.nn.relu(h @ w_up[l])) @ w_down[l]
        x = x + rmsnorm(f, g_post_ffn[l])
    return x
```

```python
import math
from contextlib import ExitStack

import numpy as np
import concourse.bass as bass
import concourse.mybir as mybir
from concourse.bass_utils import run_bass_kernel_spmd

F32 = mybir.dt.float32
BF16 = mybir.dt.bfloat16
AF = mybir.ActivationFunctionType
ALU = mybir.AluOpType

ENGS = ("pe", "act", "dve", "pool", "sp")

D = 2048
T = 4096
NB = 32
NOWN = 8
DFF = 8192
EPS = 1e-6
SCALE = 128 ** -0.5
NEGM = -30000.0


class Tok:
    __slots__ = ("name", "w", "r", "excl")

    def __init__(self, name="", excl=False):
        self.name = name
        self.w = None
        self.r = []
        self.excl = excl


class Op:
    __slots__ = ("eng", "fn", "seq", "waits", "ms", "msval", "dma", "slot", "val")


class Prog:
    def __init__(self, n_dma_sems=32):
        self.ops = {e: [] for e in ENGS}
        self.seen = {e: {} for e in ENGS}
        self.NS = n_dma_sems
        self.H = n_dma_sems // 2
        self.ndma = {"sp": 0, "pool": 0}
        self.dma_last = [None] * n_dma_sems
        self.cut = False

    def tok(self, name=""):
        return Tok(name)

    def toks(self, n, name=""):
        return [Tok(f"{name}{i}") for i in range(n)]

    def add(self, eng, fn, reads=(), writes=(), dma=False, extra_deps=()):
        if self.cut:
            return None
        ex = [t for t in reads if t.excl]
        if ex:
            reads = [t for t in reads if not t.excl]
            writes = list(writes) + [t for t in ex if t not in writes]
        op = Op()
        op.eng = eng
        op.fn = fn
        op.dma = dma
        op.ms = False
        op.msval = None
        op.seq = len(self.ops[eng])
        deps = list(extra_deps)
        for t in reads:
            if t.w is not None:
                deps.append(t.w)
        for t in writes:
            if t.w is not None:
                deps.append(t.w)
            deps.extend(t.r)
        if dma:
            assert eng in ("sp", "pool")
            nd = self.ndma[eng]
            slot = nd % self.H + (0 if eng == "sp" else self.H)
            op.slot = slot
            op.val = 16 * (nd // self.H + 1)
            if self.dma_last[slot] is not None:
                deps.append(self.dma_last[slot])
            self.dma_last[slot] = op
            self.ndma[eng] = nd + 1
        waits = []
        seen = self.seen[eng]
        for d in deps:
            if d is op:
                continue
            if d.dma:
                key = ("d", d.slot)
                lvl = d.val
            else:
                if d.eng == eng and eng == "pe":
                    continue
                key = ("c", d.eng)
                lvl = d.seq
            if seen.get(key, -1) >= lvl:
                continue
            seen[key] = lvl
            waits.append(d)
            if not d.dma:
                d.ms = True
        op.waits = waits
        for t in reads:
            t.r.append(op)
        for t in writes:
            t.w = op
            t.r = []
        self.ops[eng].append(op)
        return op

    def barrier(self):
        lasts = []
        for e in ENGS:
            for o in reversed(self.ops[e]):
                if o.fn is not None and not o.dma:
                    lasts.append(o)
                    break
        dmas = [d for d in self.dma_last if d is not None]
        for e in ENGS:
            self.add(e, None, extra_deps=lasts + dmas)

    def emit(self, block, csem, dsem):
        for e in ENGS:
            c = 0
            for op in self.ops[e]:
                if op.ms:
                    c += 1
                    op.msval = c

        def run(ename):
            def body(eng):
                for op in self.ops[ename]:
                    for d in op.waits:
                        if d.dma:
                            eng.wait_ge(dsem[d.slot], d.val)
                        else:
                            eng.wait_ge(csem[d.eng], d.msval)
                    ins = op.fn(eng) if op.fn is not None else None
                    if op.dma:
                        ins.then_inc(dsem[op.slot], 16)
                    elif op.ms:
                        assert ins is not None
                        ins.then_inc(csem[ename], 1)
            return body

        block.tensor(run("pe"))
        block.scalar(run("act"))
        block.vector(run("dve"))
        block.gpsimd(run("pool"))
        block.sync(run("sp"))


class Arena:
    def __init__(self, ap, nbytes):
        self.ap = ap
        self.n = nbytes
        self.top = 0

    def seek(self, off, limit=None):
        assert off % 64 == 0
        self.top = off
        self.limit = limit if limit is not None else self.n

    def alloc(self, shape, dtype):
        esz = 2 if dtype == BF16 else 4
        n = 1
        for s in shape:
            n *= s
        off = (self.top + 63) // 64 * 64
        nb = n * esz
        self.top = off + nb
        assert self.top <= getattr(self, "limit", self.n), f"arena overflow {self.top} > {getattr(self, 'limit', self.n)}"
        v = self.ap[:, off // 2: off // 2 + nb // 2]
        if dtype == F32:
            v = v.bitcast(F32)
        if len(shape) == 2:
            v = v.rearrange("p (a b) -> p a b", a=shape[0])
        elif len(shape) == 3:
            v = v.rearrange("p (a b c) -> p a b c", a=shape[0], b=shape[1])
        elif len(shape) == 4:
            v = v.rearrange("p (a b c d) -> p a b c d", a=shape[0], b=shape[1], c=shape[2])
        return v


def build(dbg=False, stop_after=None):
    nc = bass.Bass("TRN2", target_bir_lowering=False)

    def din(name, shape):
        return nc.dram_tensor(name, list(shape), F32, kind="ExternalInput").ap()

    xb = din("xb", [T, D])
    xo = din("xo", [1024, D])
    xh = din("xh", [16, D])
    validk_d = din("validk", [128, NB])
    rcx_d = din("rcx", [128, 2, 65])
    impmax_d = din("impmax", [128, NOWN, 64])
    impmin_d = din("impmin", [128, NOWN, 64])
    bw_d = din("bwraw", [128, 2, 2, 4, 128])
    bc_d = din("bcraw", [128, 2, 4, 4, 128])
    b31_d = din("b31", [128, 8])
    mkd_d = din("mkdiag", [128, 128])
    mkf_d = din("mkfar", [128, 128])
    mkc_d = din("mkc", [128, 4, 128])
    ident_d = din("ident", [128, 128])
    eoh_d = din("eoh", [64, NB * 128])
    gains_d = din("gains", [4, 128, D])
    wq_d = din("wq", [D, 1024])
    wkv_d = din("wkv", [D, 1536])
    wg_d = din("wg", [D, 24])
    wcv_d = din("wcv", [D, 8, 384])
    w1k_d = din("w1k", [32, 128, 256])
    w1v_d = din("w1v", [32, 128, 256])
    w2k_d = din("w2k", [256, 128])
    w2v_d = din("w2v", [256, 128])
    peT_d = din("peT", [128, 32])
    convw_d = din("convw", [128, 8, 3])
    wo_d = din("wo", [D, D])
    wup_d = din("wup", [D, DFF])
    wdn_d = din("wdn", [DFF, D])
    out_d = nc.dram_tensor("out", [1024, D], F32, kind="ExternalOutput").ap()
    x1s = nc.dram_tensor("x1s", [1024, D], F32).ap()
    dbg_outs = {}

    P = Prog()
    ARENA_BYTES = 206 * 1024
    with ExitStack() as es:
        E = es.enter_context
        arena_t = E(nc.sbuf_tensor("arena", [128, ARENA_BYTES // 2], BF16))
        ps_t = E(nc.psum_tensor("ps", [128, 4096], F32))
        csem = {e: E(nc.semaphore("c_" + e)) for e in ENGS}
        dsem = [E(nc.semaphore(f"d{i}")) for i in range(P.NS)]
        block = E(nc.Block())

        A = Arena(arena_t[:], ARENA_BYTES)
        PS = ps_t[:]
        PSB = PS.bitcast(BF16)

        def bank(b, n=512):
            return PS[:, b * 512: b * 512 + n]

        def bankb(b, n=1024):
            return PSB[:, b * 1024: b * 1024 + n]

        tP = [Tok(f"ps{i}", excl=True) for i in range(8)]

        def dbg_dump(name, ap, shape, dtype=F32):
            if not dbg:
                return
            P.barrier()
            o = nc.dram_tensor("dbg_" + name, list(shape), dtype, kind="ExternalOutput").ap()
            dbg_outs[name] = o
            P.add("sp", lambda e: e.dma_start(out=o, in_=ap), dma=True)

        ident = A.alloc([128], BF16)
        scr = A.alloc([512], F32)
        gA = A.alloc([D], F32)
        OFF_XIN = A.top
        xin = [A.alloc([D], F32) for _ in range(2)]
        OFF_XN = A.top
        xn = [A.alloc([D], BF16) for _ in range(2)]
        assert A.top == 35072, A.top
        t_xin = P.toks(2, "xin")
        t_xn = P.toks(2, "xn")
        ksT = A.alloc([2, T], BF16)
        kwT = A.alloc([2, T], BF16)
        vsA = A.alloc([NB, 2, 129], BF16)
        vwA = A.alloc([NB, 2, 129], BF16)
        kcmpT = A.alloc([2, 256], BF16)
        RC = A.alloc([2, 2, 193], BF16)
        assert A.top <= 103488, A.top
        A.seek(103488)
        kcT = A.alloc([2, 16, 256], BF16)
        vcT = A.alloc([2, 16, 256], BF16)
        assert A.top == 136256

        scr_i = [0]

        def scol(n=1):
            c = scr_i[0]
            scr_i[0] += n
            assert scr_i[0] <= 512
            return scr[:, c:c + n]

        t_set = P.tok("setup")
        tmpf = xin[1]
        P.add("sp", lambda e: e.dma_start(out=tmpf[:, 0:128], in_=ident_d), writes=[t_set], dma=True)
        P.add("dve", lambda e: e.tensor_copy(out=ident, in_=tmpf[:, 0:128]), reads=[t_set], writes=[t_set])
        P.add("sp", lambda e: e.dma_start(out=tmpf[:, 128:160], in_=validk_d), writes=[t_set], dma=True)
        for tens in (vsA, vwA):
            for g in range(2):
                P.add("dve", lambda e, tens=tens, g=g: e.tensor_copy(out=tens[:, :, g, 128], in_=tmpf[:, 128:160]),
                      reads=[t_set], writes=[t_set])
        P.add("sp", lambda e: e.dma_start(out=gA, in_=gains_d[0]), writes=[t_set], dma=True)
        P.add("pool", lambda e: e.memset(kcmpT, 0.0), writes=[t_set])
        P.add("pool", lambda e: e.memset(RC, 0.0), writes=[t_set])
        P.barrier()

        def rms_sq(src, t_src, junk, t_junk, np_=128):
            c = scol(4)
            t_c = P.tok("c")
            P.add("act", lambda e: e.activation(out=junk[:np_], in_=src, func=AF.Square, accum_out=c[:np_, 0:1]),
                  reads=[t_src], writes=[t_junk, t_c])
            return c, t_c

        def rms_fin(c, t_c, np_=128):
            P.add("dve", lambda e: e.tensor_scalar(out=c[:np_, 1:2], in0=c[:np_, 0:1], scalar1=1.0 / D, scalar2=EPS,
                                                   op0=ALU.mult, op1=ALU.add), reads=[t_c], writes=[t_c])
            P.add("act", lambda e: e.activation(out=c[:np_, 2:3], in_=c[:np_, 1:2], func=AF.Sqrt), reads=[t_c], writes=[t_c])
            P.add("dve", lambda e: e.reciprocal(out=c[:np_, 3:4], in_=c[:np_, 2:3]), reads=[t_c], writes=[t_c])

        def rms_to_xn(src, t_src, gt, xnb, t_xnb, np_=128):
            c, t_c = rms_sq(src, t_src, xnb, t_xnb, np_)
            rms_fin(c, t_c, np_)
            return c, t_c

        def pipeline(n, stages):
            K_ = len(stages)
            for step in range(n + K_ - 1):
                for k in range(K_ - 1, -1, -1):
                    it = step - k
                    if 0 <= it < n:
                        stages[k](it)

        nstate = {}

        def norm_s0(key, src_dram, xi, n_rows=128):
            xin_b, xn_b = xin[xi], xn[xi]
            P.add("sp", lambda e: e.dma_start(out=xin_b[:n_rows], in_=src_dram), writes=[t_xin[xi]], dma=True)
            nstate[key] = rms_sq(xin_b[:n_rows], t_xin[xi], xn_b, t_xn[xi], n_rows)

        def norm_s1(key, gt, xi, n_rows=128):
            xin_b, xn_b = xin[xi], xn[xi]
            c, t_c = nstate.pop(key)
            rms_fin(c, t_c, n_rows)
            P.add("dve", lambda e: e.scalar_tensor_tensor(out=xn_b[:n_rows], in0=xin_b[:n_rows], scalar=c[:n_rows, 3:4],
                                                          in1=gt[:n_rows], op0=ALU.mult, op1=ALU.mult),
                  reads=[t_xin[xi], t_c], writes=[t_xn[xi]])

        def norm_block(src_dram, gt, xi, n_rows=128):
            norm_s0(("nb", xi), src_dram, xi, n_rows)
            norm_s1(("nb", xi), gt, xi, n_rows)

        def transpose_block(xi, pb, dst, t_dst, evac, n_rows=128):
            xn_b = xn[xi]
            pv = PSB[:, pb * 1024: pb * 1024 + 16 * n_rows].rearrange("p (k t) -> p k t", k=16)

            def tr(e):
                for k in range(16):
                    ins = e.transpose(out=pv[:, k, :], in_=xn_b[:n_rows, k * 128:(k + 1) * 128],
                                      identity=ident[:n_rows, :n_rows])
                return ins
            P.add("pe", tr, reads=[t_xn[xi]], writes=[tP[pb], tP[pb + 1]])
            if evac == "act":
                P.add("act", lambda e: e.copy(out=dst, in_=pv), reads=[tP[pb], tP[pb + 1]], writes=[t_dst])
            else:
                P.add("dve", lambda e: e.tensor_copy(out=dst, in_=pv), reads=[tP[pb], tP[pb + 1]], writes=[t_dst])

        wkv = A.alloc([16, 1536], BF16)
        hT = [A.alloc([16, 256], BF16) for _ in range(2)]
        t_hT = P.toks(2, "hT")
        t_wkv = P.toks(4, "wkv")
        for cg in range(4):
            P.add("pool", lambda e, cg=cg: e.dma_start(
                out=wkv[:, :, cg * 384:(cg + 1) * 384],
                in_=wkv_d[:, cg * 384:(cg + 1) * 384].rearrange("(k p) n -> p k n", p=128)),
                writes=[t_wkv[cg]], dma=True)
        fdst = [(kcT, 0), (kcT, 1), (vcT, 0), (vcT, 1), (ksT, 0), (ksT, 1), (kwT, 0), (kwT, 1)]
        evc = [0]

        def a1_kv(tt):
            hb = hT[tt % 2]
            for fg in range(8):
                pb = 4 + evc[0] % 4
                evc[0] += 1
                dst, g = fdst[fg]

                def mm(e, fg=fg, hb=hb, pb=pb):
                    for k in range(16):
                        ins = e.matmul(bank(pb, 256), lhsT=wkv[:, k, fg * 128:(fg + 1) * 128], rhs=hb[:, k, :],
                                       start=(k == 0), stop=(k == 15))
                    return ins
                P.add("pe", mm, reads=[t_hT[tt % 2], t_wkv[(fg * 128) // 384]], writes=[tP[pb]])
                if fg < 4:
                    dv = dst[:, g, :, tt * 16:(tt + 1) * 16].rearrange("p r m -> p m r")
                    sv = bank(pb, 256).rearrange("p (m r) -> p m r", r=16)
                else:
                    dv = dst[:, g, tt * 256:(tt + 1) * 256]
                    sv = bank(pb, 256)
                if evc[0] % 2 == 0:
                    P.add("act", lambda e, dv=dv, sv=sv: e.copy(out=dv, in_=sv), reads=[tP[pb]], writes=[])
                else:
                    P.add("dve", lambda e, dv=dv, sv=sv: e.tensor_copy(out=dv, in_=sv), reads=[tP[pb]], writes=[])
            for sb in range(2):
                blk = 2 * tt + sb
                pb = 4 + evc[0] % 4
                evc[0] += 1

                def mm2(e, hb=hb, sb=sb, pb=pb):
                    for k in range(16):
                        ins = e.matmul(bank(pb), lhsT=hb[:, k, sb * 128:(sb + 1) * 128], rhs=wkv[:, k, 1024:1536],
                                       start=(k == 0), stop=(k == 15))
                    return ins
                P.add("pe", mm2, reads=[t_hT[tt % 2], t_wkv[2], t_wkv[3]], writes=[tP[pb]])
                pv = bank(pb).rearrange("p (a g d) -> p a g d", a=2, g=2)
                if blk % 2 == 0:
                    P.add("act", lambda e, blk=blk, pv=pv: e.copy(out=vsA[:, blk, :, 0:128], in_=pv[:, 0]), reads=[tP[pb]], writes=[])
                    P.add("act", lambda e, blk=blk, pv=pv: e.copy(out=vwA[:, blk, :, 0:128], in_=pv[:, 1]), reads=[tP[pb]], writes=[])
                else:
                    P.add("dve", lambda e, blk=blk, pv=pv: e.tensor_copy(out=vsA[:, blk, :, 0:128], in_=pv[:, 0]), reads=[tP[pb]], writes=[])
                    P.add("dve", lambda e, blk=blk, pv=pv: e.tensor_copy(out=vwA[:, blk, :, 0:128], in_=pv[:, 1]), reads=[tP[pb]], writes=[])

        def a1_s0(blk):
            norm_s0(("a1", blk), xb[blk * 128:(blk + 1) * 128, :], blk % 2)

        def a1_s1(blk):
            norm_s1(("a1", blk), gA, blk % 2)

        def a1_s2(blk):
            tt, sb = blk // 2, blk % 2
            transpose_block(blk % 2, 2 * (blk % 2), hT[tt % 2][:, :, sb * 128:(sb + 1) * 128], t_hT[tt % 2],
                            "act" if blk % 2 == 0 else "dve")

        def a1_s3(blk):
            if blk % 2 == 1:
                a1_kv(blk // 2)
        for step in range(NB + 3):
            if 0 <= step - 2 < NB:
                a1_s2(step - 2)
            if 0 <= step - 1 < NB:
                a1_s1(step - 1)
            if step < NB:
                a1_s0(step)
            if 0 <= step - 3 < NB:
                a1_s3(step - 3)
        P.barrier()
        A.seek(136256)

        if stop_after == "A1":
            P.cut = True
        W1 = [A.alloc([32, 256], BF16) for _ in range(2)]
        W2 = [A.alloc([2, 128], BF16) for _ in range(2)]
        peT = A.alloc([32], BF16)
        hidT = [A.alloc([2, 256], BF16) for _ in range(2)]
        cbias = A.alloc([4], F32)
        rcx = A.alloc([2, 65], F32)
        t_a2 = P.tok("a2w")
        t_w1 = P.toks(2, "w1")
        P.add("pool", lambda e: e.dma_start(out=peT, in_=peT_d), writes=[t_a2], dma=True)
        for ty, (w1d, w2d) in enumerate(((w1k_d, w2k_d), (w1v_d, w2v_d))):
            for l4 in range(4):
                P.add("pool", lambda e, ty=ty, w1d=w1d, l4=l4: e.dma_start(
                    out=W1[ty][:, l4 * 8:(l4 + 1) * 8, :], in_=w1d[l4 * 8:(l4 + 1) * 8].rearrange("l d h -> d l h")),
                    writes=[t_w1[ty]], dma=True)
            P.add("pool", lambda e, ty=ty, w2d=w2d: e.dma_start(
                out=W2[ty], in_=w2d.rearrange("(c p) e -> p c e", p=128)), writes=[t_w1[ty]], dma=True)
        P.add("sp", lambda e: e.dma_start(out=rcx, in_=rcx_d), writes=[t_a2], dma=True)
        for g in range(2):
            P.add("dve", lambda e, g=g: e.tensor_copy(out=RC[:, :, g, 128:193], in_=rcx), reads=[t_a2], writes=[])
        t_hid = P.toks(2, "hid")
        t_cb = P.tok("cb")
        pbi = 0
        for ty in range(2):
            src = kcT if ty == 0 else vcT
            pbb = 7

            def mmb(e, ty=ty):
                for hh in range(2):
                    for l in range(32):
                        ins = e.matmul(bank(7)[:, hh:hh + 1], lhsT=W1[ty][:, l, hh * 128:(hh + 1) * 128], rhs=peT[:, l:l + 1],
                                       start=(l == 0), stop=(l == 31))
                return ins
            P.add("pe", mmb, reads=[t_a2, t_w1[ty]], writes=[tP[7]])
            P.add("dve", lambda e, ty=ty: e.tensor_copy(out=cbias[:, 2 * ty:2 * ty + 2], in_=bank(7)[:, 0:2]),
                  reads=[tP[7]], writes=[t_cb])
            for g in range(2):
                hd = hidT[g]
                for hh in range(2):
                    pb = pbi % 4
                    pbi += 1

                    def mmc(e, ty=ty, g=g, hh=hh, pb=pb, src=src):
                        for l in range(32):
                            ins = e.matmul(bank(pb, 255), lhsT=W1[ty][:, l, hh * 128:(hh + 1) * 128],
                                           rhs=src[:, g, l % 16, l // 16:l // 16 + 255], start=(l == 0), stop=(l == 31))
                        return ins
                    P.add("pe", mmc, reads=[t_w1[ty]], writes=[tP[pb]])
                    P.add("act", lambda e, ty=ty, hh=hh, pb=pb, hd=hd: e.activation(
                        out=hd[:, hh, 0:255], in_=bank(pb, 255), func=AF.Silu, bias=cbias[:, 2 * ty + hh:2 * ty + hh + 1]),
                        reads=[tP[pb], t_cb], writes=[t_hid[g]])
                if ty == 0:
                    pb = 4 + g

                    def mmk(e, hd=hd, pb=pb):
                        for hh in range(2):
                            ins = e.matmul(bank(pb, 255), lhsT=W2[0][:, hh, :], rhs=hd[:, hh, 0:255],
                                           start=(hh == 0), stop=(hh == 1))
                        return ins
                    P.add("pe", mmk, reads=[t_hid[g], t_w1[0]], writes=[tP[pb]])
                    P.add("dve", lambda e, g=g, pb=pb: e.tensor_copy(out=kcmpT[:, g, 0:255], in_=bank(pb, 255)),
                          reads=[tP[pb]], writes=[tP[pb]])
                else:
                    for ch in range(2):
                        n = 128 if ch == 0 else 127
                        pb = 4 + (g * 2 + ch) % 2

                        def mmv(e, hd=hd, pb=pb, ch=ch, n=n):
                            for hh in range(2):
                                ins = e.matmul(bank(pb, 128)[:n], lhsT=hd[:, hh, ch * 128:ch * 128 + n], rhs=W2[1][:, hh, :],
                                               start=(hh == 0), stop=(hh == 1))
                            return ins
                        P.add("pe", mmv, reads=[t_hid[g], t_w1[1]], writes=[tP[pb]])
                        P.add("dve", lambda e, g=g, pb=pb, ch=ch, n=n: e.tensor_scalar(
                            out=RC[:n, ch, g, 0:128], in0=bank(pb, 128)[:n], scalar1=rcx[:n, ch, 0:1], scalar2=None,
                            op0=ALU.mult), reads=[tP[pb], t_a2], writes=[tP[pb]])
        dbg_dump("ksT", ksT.bitcast(F32) if False else ksT, [128, 2, T], BF16)
        dbg_dump("kcmpT", kcmpT, [128, 2, 256], BF16)
        dbg_dump("RC", RC, [128, 2, 2, 193], BF16)
        dbg_dump("vsA", vsA, [128, NB, 2, 129], BF16)
        P.barrier()

        if stop_after == "A2":
            P.cut = True
        A.seek(194560)
        ocT = A.alloc([8, 1024], BF16)
        A.seek(178176, 194560)
        oaT = A.alloc([8, 1024], BF16)
        A.seek(178176, 194560)
        wbuf0 = A.alloc([16, 384], BF16)
        A.seek(103488, 178176)
        qT = A.alloc([8, 1024], BF16)
        gates = A.alloc([NOWN, 24], F32)
        convw = A.alloc([8, 3], F32)
        OFF_BC = (A.top + 63) // 64 * 64
        hTo = A.alloc([16, 1024], BF16)
        hTh = A.alloc([16, 16], BF16)
        wbuf = [wbuf0, A.alloc([16, 384], BF16)]
        wgb = A.alloc([16, 24], BF16)
        ccs = A.alloc([512], F32)
        ubuf = A.alloc([4, 130], F32)
        ybuf = A.alloc([4, 128], F32)
        ybuf2 = A.alloc([4, 128], F32)
        hcs = A.alloc([16], F32)
        t_hTo = P.tok("hTo")
        t_wbuf = P.toks(2, "wbuf")
        t_misc = P.tok("miscB")
        P.add("sp", lambda e: e.dma_start(out=convw, in_=convw_d), writes=[t_misc], dma=True)
        P.add("pool", lambda e: e.dma_start(out=wgb, in_=wg_d.rearrange("(k p) n -> p k n", p=128)), writes=[t_misc], dma=True)
        units = [("q", u) for u in range(4)] + [("c", c) for c in range(8)]

        def load_unit(ui):
            kind, idx = units[ui]
            wb = wbuf[ui % 2]
            if kind == "q":
                for q2 in range(2):
                    P.add("pool", lambda e, wb=wb, idx=idx, q2=q2: e.dma_start(
                        out=wb[:, q2 * 8:(q2 + 1) * 8, 0:256],
                        in_=wq_d[q2 * 1024:(q2 + 1) * 1024, idx * 256:(idx + 1) * 256].rearrange("(k p) n -> p k n", p=128)),
                        writes=[t_wbuf[ui % 2]], dma=True)
            else:
                for q2 in range(2):
                    P.add("pool", lambda e, wb=wb, idx=idx, q2=q2: e.dma_start(
                        out=wb[:, q2 * 8:(q2 + 1) * 8, 0:384],
                        in_=wcv_d[q2 * 1024:(q2 + 1) * 1024, idx, :].rearrange("(k p) n -> p k n", p=128)),
                        writes=[t_wbuf[ui % 2]], dma=True)
        load_unit(0)
        load_unit(1)
        pipeline(NOWN, [
            lambda i: norm_s0(("b", i), xo[i * 128:(i + 1) * 128, :], i % 2),
            lambda i: norm_s1(("b", i), gA, i % 2),
            lambda i: transpose_block(i % 2, 2 * (i % 2), hTo[:, :, i * 128:(i + 1) * 128], t_hTo, "act" if i % 2 == 0 else "dve"),
        ])
        norm_block(xh, gA, 0, n_rows=16)
        t_hTh = P.tok("hTh")
        transpose_block(0, 0, hTh, t_hTh, "act", n_rows=16)
        t_gates = P.tok("gates")
        for i in range(NOWN):
            pb = 4 + i % 2

            def mmg(e, i=i, pb=pb):
                for k in range(16):
                    ins = e.matmul(bank(pb, 24), lhsT=hTo[:, k, i * 128:(i + 1) * 128], rhs=wgb[:, k, :],
                                   start=(k == 0), stop=(k == 15))
                return ins
            P.add("pe", mmg, reads=[t_hTo, t_misc], writes=[tP[pb]])
            P.add("act", lambda e, i=i, pb=pb: e.activation(out=gates[:, i, :], in_=bank(pb, 24), func=AF.Sigmoid),
                  reads=[tP[pb]], writes=[t_gates])
        t_ccs, t_u, t_y, t_hcs, t_y2 = P.toks(5, "cv")
        pbi = 0
        cbi = [0]
        for ui, (kind, idx) in enumerate(units):
            wb = wbuf[ui % 2]
            t_wb = t_wbuf[ui % 2]
            if kind == "q":
                for hl in range(2):
                    for half in range(2):
                        pb = 4 + pbi % 4
                        pbi += 1

                        def mmq(e, wb=wb, hl=hl, half=half, pb=pb):
                            for k in range(16):
                                ins = e.matmul(bank(pb), lhsT=wb[:, k, hl * 128:(hl + 1) * 128],
                                               rhs=hTo[:, k, half * 512:(half + 1) * 512], start=(k == 0), stop=(k == 15))
                            return ins
                        P.add("pe", mmq, reads=[t_hTo, t_wb], writes=[tP[pb]])
                        dv = qT[:, idx * 2 + hl, half * 512:(half + 1) * 512]
                        if pbi % 2 == 0:
                            P.add("act", lambda e, dv=dv, pb=pb: e.copy(out=dv, in_=bank(pb)), reads=[tP[pb]], writes=[tP[pb]])
                        else:
                            P.add("dve", lambda e, dv=dv, pb=pb: e.tensor_copy(out=dv, in_=bank(pb)), reads=[tP[pb]], writes=[tP[pb]])
            else:
                cch = idx
                def mmh(e, wb=wb):
                    for s, ty in enumerate((0, 2)):
                        for k in range(16):
                            ins = e.matmul(bank(3)[:, s * 16:(s + 1) * 16], lhsT=wb[:, k, ty * 128:(ty + 1) * 128], rhs=hTh[:, k, :],
                                           start=(k == 0), stop=(k == 15))
                    return ins
                P.add("pe", mmh, reads=[t_hTh, t_wb], writes=[tP[3]])
                P.add("act", lambda e: e.copy(out=hcs, in_=bank(3)[:, 16:32]), reads=[tP[3]], writes=[t_hcs])
                for half in range(2):
                    pbs = []
                    for ty in range(3):
                        pb = (0, 1, 2, 4, 5, 6)[cbi[0] % 6]
                        cbi[0] += 1
                        pbs.append(pb)

                        def mmc2(e, wb=wb, ty=ty, half=half, pb=pb):
                            for k in range(16):
                                ins = e.matmul(bank(pb), lhsT=wb[:, k, ty * 128:(ty + 1) * 128],
                                               rhs=hTo[:, k, half * 512:(half + 1) * 512], start=(k == 0), stop=(k == 15))
                            return ins
                        P.add("pe", mmc2, reads=[t_hTo, t_wb], writes=[tP[pb]])
                    p_ch, p_cb, p_cc = pbs
                    P.add("act", lambda e, p_cc=p_cc: e.copy(out=ccs, in_=bank(p_cc)), reads=[tP[p_cc]], writes=[t_ccs, tP[p_cc]])
                    P.add("dve", lambda e, p_ch=p_ch: e.tensor_tensor(
                        out=ubuf[:, :, 2:130], in0=bank(p_ch).rearrange("p (a b) -> p a b", a=4),
                        in1=ccs.rearrange("p (a b) -> p a b", a=4), op=ALU.mult),
                        reads=[tP[p_ch], t_ccs], writes=[t_u, tP[p_ch]])
                    P.add("dve", lambda e, half=half: e.tensor_tensor(
                        out=ubuf[:, :, 0:2], in0=bank(3)[:, half * 8:(half + 1) * 8].rearrange("p (a b) -> p a b", a=4),
                        in1=hcs[:, half * 8:(half + 1) * 8].rearrange("p (a b) -> p a b", a=4), op=ALU.mult),
                        reads=[tP[3], t_hcs], writes=[t_u])
                    P.add("dve", lambda e, cch=cch: e.tensor_scalar(out=ybuf, in0=ubuf[:, :, 0:128], scalar1=convw[:, cch, 0:1],
                                                                   scalar2=None, op0=ALU.mult),
                          reads=[t_u, t_misc], writes=[t_y])
                    for kk in (1, 2):
                        P.add("dve", lambda e, cch=cch, kk=kk: e.scalar_tensor_tensor(
                            out=ybuf, in0=ubuf[:, :, kk:kk + 128], scalar=convw[:, cch, kk:kk + 1], in1=ybuf,
                            op0=ALU.mult, op1=ALU.add), reads=[t_u, t_y], writes=[t_y])
                    P.add("dve", lambda e, cch=cch, half=half, p_cb=p_cb: e.tensor_tensor(
                        out=ocT[:, cch, half * 512:(half + 1) * 512].rearrange("p (a b) -> p a b", a=4),
                        in0=ybuf, in1=bank(p_cb).rearrange("p (a b) -> p a b", a=4), op=ALU.mult),
                        reads=[t_y, tP[p_cb]], writes=[tP[p_cb]])
            if ui + 2 < len(units):
                load_unit(ui + 2)
        dbg_dump("qT", qT, [128, 8, 1024], BF16)
        dbg_dump("ocT", ocT, [128, 8, 1024], BF16)
        dbg_dump("gates", gates, [128, NOWN, 24])
        P.barrier()

        if stop_after == "B":
            P.cut = True
        A.seek(OFF_XN, 35072)
        TW = A.alloc([2, 2, 4, 128], F32)
        A.seek(OFF_XIN, OFF_XN)
        TC = A.alloc([2, 4, 4, 128], F32)
        A.seek(159744, 178176)
        wob0 = A.alloc([16, 512], BF16)
        t_wob = P.toks(2, "wob")
        t_oaoc = P.tok("oaoc")
        A.seek(OFF_BC, 159744)
        mkd = A.alloc([128], F32)
        mkf = A.alloc([128], F32)
        mkc = A.alloc([4, 128], F32)
        b31 = A.alloc([8], F32)
        impmax = A.alloc([NOWN, 64], F32)
        impmin = A.alloc([NOWN, 64], F32)
        Ebuf = [A.alloc([512], F32) for _ in range(2)]
        pbuf = [A.alloc([4, 128], BF16) for _ in range(4)]
        Eoh = A.alloc([NB, 128], BF16)
        nselT = [A.alloc([128], BF16) for _ in range(2)]
        selb = A.alloc([64], BF16)
        impb = [A.alloc([64], F32) for _ in range(2)]
        m8 = A.alloc([16], F32)
        rden = [A.alloc([3, 4], F32) for _ in range(2)]
        cf = [A.alloc([3, 4], F32) for _ in range(2)]
        dtmp = A.alloc([4], F32)
        oacc = [A.alloc([4, 128], F32) for _ in range(2)]
        oab = A.alloc([4, 128], BF16)
        accCs = A.alloc([2, 386], F32)
        accSs = A.alloc([2, 258], F32)
        t_tab = P.tok("tab")
        for (dst, srcd) in ((TW, bw_d.rearrange("k g a h t -> k (g a h t)")), (TC, bc_d.rearrange("k g a h t -> k (g a h t)")),
                            (mkd, mkd_d), (mkf, mkf_d), (mkc, mkc_d.rearrange("k a t -> k (a t)")), (b31, b31_d),
                            (impmax, impmax_d.rearrange("k a t -> k (a t)")), (impmin, impmin_d.rearrange("k a t -> k (a t)"))):
            dflat = dst
            nd = len(dst.shape)
            if nd == 3:
                dflat = dst.rearrange("p a b -> p (a b)")
            elif nd == 5:
                dflat = dst.rearrange("p a b c d -> p (a b c d)")
            P.add("sp", lambda e, dflat=dflat, srcd=srcd: e.dma_start(out=dflat, in_=srcd), writes=[t_tab], dma=True)
        P.add("pool", lambda e: e.dma_start(out=Eoh[0:64].rearrange("p a b -> p (a b)"), in_=eoh_d), writes=[t_tab], dma=True)
        P.add("dve", lambda e: e.memset(Eoh[64:128], 0.0), writes=[t_tab])
        for pp_ in range(2):
            P.add("dve", lambda e, pp_=pp_: e.memset(nselT[pp_][64:128], 0.0), writes=[t_tab])
        for g in range(2):
            for h in range(4):
                bcol = b31[:, g * 4 + h:g * 4 + h + 1]
                P.add("dve", lambda e, g=g, h=h, bcol=bcol: e.scalar_tensor_tensor(
                    out=TW[:, g, 0, h, :], in0=TW[:, g, 0, h, :], scalar=bcol, in1=mkd, op0=ALU.subtract, op1=ALU.add),
                    reads=[t_tab], writes=[t_tab])
                P.add("dve", lambda e, g=g, h=h, bcol=bcol: e.tensor_scalar(
                    out=TW[:, g, 1, h, :], in0=TW[:, g, 1, h, :], scalar1=bcol, scalar2=None, op0=ALU.subtract),
                    reads=[t_tab], writes=[t_tab])
                for off in range(4):
                    P.add("dve", lambda e, g=g, h=h, off=off, bcol=bcol: e.scalar_tensor_tensor(
                        out=TC[:, g, off, h, :], in0=TC[:, g, off, h, :], scalar=bcol, in1=mkc[:, off, :],
                        op0=ALU.subtract, op1=ALU.add), reads=[t_tab], writes=[t_tab])
        dbg_dump("TW", TW, [128, 2, 2, 4, 128])
        dbg_dump("TC", TC, [128, 2, 4, 4, 128])
        P.barrier()
        for q4 in range(4):
            P.add("pool", lambda e, q4=q4: e.dma_start(
                out=wob0[:, q4 * 4:(q4 + 1) * 4, :],
                in_=wo_d[q4 * 512:(q4 + 1) * 512, 0:512].rearrange("(k p) n -> p k n", p=128)),
                writes=[t_wob[0]], dma=True)

        NIG = 2 * NOWN
        SB = (0, 1, 6)
        t_E = P.toks(2, "E")
        t_p = P.toks(4, "p")
        t_selb, t_m8, t_oab = P.toks(3, "at")
        t_nsel = P.toks(2, "nsel")
        t_imp = P.toks(2, "imp")
        t_rden = P.toks(2, "rden")
        t_cf = P.toks(2, "cf")
        t_oacc = P.toks(2, "oacc")
        t_dtmp = P.tok("dtmp")
        t_accCs = P.tok("accCs")
        t_accSs = P.tok("accSs")
        ecnt = [0]

        def acc_view(b0, w):
            return [bank(b0 + h // 2)[:, (h % 2) * w:(h % 2) * w + w] for h in range(4)]

        deferred = []

        def flush(key):
            for d in [d for d in deferred if d[1] == key]:
                d[2]()
                deferred.remove(d)

        def qslice(n):
            i, g = n // 2, n % 2
            return qT[:, 4 * g:4 * g + 4, i * 128:(i + 1) * 128]

        def post_C(n, step):
            i, g = n // 2, n % 2
            pp = n % 2
            for bb in range(2):
                P.add("dve", lambda e, bb=bb: e.tensor_copy(out=accCs[:, bb, :], in_=bank(2 + bb)[:, 0:386]),
                      reads=[tP[2 + bb]], writes=[t_accCs])
            accC = [accCs[:, h // 2, (h % 2) * 193:(h % 2) * 193 + 193] for h in range(4)]
            for bb in range(2):
                P.add("dve", lambda e, bb=bb: e.tensor_scalar_max(
                    out=dtmp[:, 2 * bb:2 * bb + 2],
                    in0=accCs[:, bb, :].rearrange("p (a w) -> p a w", a=2)[:, :, 128], scalar1=1e-30),
                    reads=[t_accCs], writes=[t_dtmp])
                P.add("dve", lambda e, bb=bb: e.reciprocal(out=rden[pp][:, 0, 2 * bb:2 * bb + 2], in_=dtmp[:, 2 * bb:2 * bb + 2]),
                      reads=[t_dtmp], writes=[t_rden[pp]])
            gv = gates[:, i, g * 12:(g + 1) * 12].rearrange("p (h b) -> p h b", b=3)
            P.add("dve", lambda e: e.tensor_tensor(out=cf[pp][:, 0, :], in0=gv[:, :, 0], in1=rden[pp][:, 0, :], op=ALU.mult),
                  reads=[t_rden[pp]], writes=[t_cf[pp]])

            def evac_c():
                for h in range(4):
                    P.add("act", lambda e, h=h: e.activation(out=oacc[pp][:, h, :], in_=accC[h][:, 0:128], func=AF.Copy,
                                                             scale=cf[pp][:, 0, h:h + 1]),
                          reads=[t_accCs, t_cf[pp]], writes=[t_oacc[pp]])
            deferred.append((step + 2, ("c", n), evac_c))
            ib = impb[0]
            P.add("dve", lambda e: e.tensor_scalar(out=ib, in0=accC[0][:, 129:193], scalar1=rden[pp][:, 0, 0:1],
                                                   scalar2=None, op0=ALU.mult),
                  reads=[t_accCs, t_rden[pp]], writes=[t_imp[0]])
            for h in range(1, 4):
                P.add("dve", lambda e, h=h: e.scalar_tensor_tensor(
                    out=ib, in0=accC[h][:, 129:193], scalar=rden[pp][:, 0, h:h + 1], in1=ib, op0=ALU.mult, op1=ALU.add),
                    reads=[t_accCs, t_rden[pp], t_imp[0]], writes=[t_imp[0]])
            P.add("dve", lambda e: e.tensor_tensor(out=ib, in0=ib, in1=impmax[:, i, :], op=ALU.max),
                  reads=[t_imp[0], t_tab], writes=[t_imp[0]])
            P.add("dve", lambda e: e.tensor_tensor(out=ib, in0=ib, in1=impmin[:, i, :], op=ALU.min),
                  reads=[t_imp[0]], writes=[t_imp[0]])
            P.add("dve", lambda e: e.max(out=m8[:, 0:8], in_=ib), reads=[t_imp[0]], writes=[t_m8])
            P.add("dve", lambda e: e.match_replace(out=impb[1], in_to_replace=m8[:, 0:8], in_values=ib, imm_value=-3.0e38),
                  reads=[t_imp[0], t_m8], writes=[t_imp[1]])
            P.add("dve", lambda e: e.max(out=m8[:, 8:16], in_=impb[1]), reads=[t_imp[1]], writes=[t_m8])
            P.add("dve", lambda e: e.tensor_tensor(out=selb, in0=ib, in1=m8[:, 15:16].to_broadcast([128, 64]), op=ALU.is_ge),
                  reads=[t_imp[0], t_m8], writes=[t_selb])

            def tr_sel():
                tv = bankb(7, 128)[0:64]

                def trs(e):
                    return e.transpose(out=tv, in_=selb, identity=ident)
                P.add("pe", trs, reads=[t_selb], writes=[tP[7]])
                P.add("dve", lambda e: e.tensor_scalar(out=nselT[pp][0:64], in0=tv, scalar1=-1.0, scalar2=-NEGM / SCALE,
                                                       op0=ALU.add, op1=ALU.mult),
                      reads=[tP[7]], writes=[t_nsel[pp]])
            deferred.append((step + 5, ("sel", n), tr_sel))

        def post_W(n):
            i, g = n // 2, n % 2
            pp = n % 2
            flush(("c", n))
            accW = acc_view(4, 129)
            gv = gates[:, i, g * 12:(g + 1) * 12].rearrange("p (h b) -> p h b", b=3)
            for bb in range(2):
                P.add("dve", lambda e, bb=bb: e.reciprocal(
                    out=rden[pp][:, 1, 2 * bb:2 * bb + 2],
                    in_=bank(4 + bb)[:, 0:258].rearrange("p (a w) -> p a w", a=2)[:, :, 128]),
                    reads=[tP[4 + bb]], writes=[t_rden[pp]])
            P.add("dve", lambda e: e.tensor_tensor(out=cf[pp][:, 1, :], in0=gv[:, :, 2], in1=rden[pp][:, 1, :], op=ALU.mult),
                  reads=[t_rden[pp]], writes=[t_cf[pp]])
            for h in range(4):
                P.add("dve", lambda e, h=h: e.scalar_tensor_tensor(
                    out=oacc[pp][:, h, :], in0=accW[h][:, 0:128], scalar=cf[pp][:, 1, h:h + 1], in1=oacc[pp][:, h, :],
                    op0=ALU.mult, op1=ALU.add),
                    reads=[tP[4 + h // 2], t_cf[pp], t_oacc[pp]], writes=[t_oacc[pp]])

        def post_S(n, step):
            i, g = n // 2, n % 2
            pp = n % 2
            for bb in range(2):
                P.add("dve", lambda e, bb=bb: e.tensor_copy(out=accSs[:, bb, :], in_=bank(2 + bb)[:, 0:258]),
                      reads=[tP[2 + bb]], writes=[t_accSs])
            accS = [accSs[:, h // 2, (h % 2) * 129:(h % 2) * 129 + 129] for h in range(4)]
            gv = gates[:, i, g * 12:(g + 1) * 12].rearrange("p (h b) -> p h b", b=3)
            for bb in range(2):
                P.add("dve", lambda e, bb=bb: e.reciprocal(
                    out=rden[pp][:, 2, 2 * bb:2 * bb + 2],
                    in_=accSs[:, bb, :].rearrange("p (a w) -> p a w", a=2)[:, :, 128]),
                    reads=[t_accSs], writes=[t_rden[pp]])
            P.add("dve", lambda e: e.tensor_tensor(out=cf[pp][:, 2, :], in0=gv[:, :, 1], in1=rden[pp][:, 2, :], op=ALU.mult),
                  reads=[t_rden[pp]], writes=[t_cf[pp]])
            for h in range(4):
                P.add("dve", lambda e, h=h: e.scalar_tensor_tensor(
                    out=oab[:, h, :], in0=accS[h][:, 0:128], scalar=cf[pp][:, 2, h:h + 1], in1=oacc[pp][:, h, :],
                    op0=ALU.mult, op1=ALU.add),
                    reads=[t_accSs, t_cf[pp], t_oacc[pp]], writes=[t_oab] + ([t_oacc[pp]] if h == 3 else []))

            def tr_out():
                tv = bankb(7, 512).rearrange("p (h t) -> p h t", h=4)

                def tro(e):
                    for h in range(4):
                        ins = e.transpose(out=tv[:, h, :], in_=oab[:, h, :], identity=ident)
                    return ins
                P.add("pe", tro, reads=[t_oab], writes=[tP[7]])
                P.add("act", lambda e: e.copy(out=oaT[:, 4 * g:4 * g + 4, i * 128:(i + 1) * 128], in_=tv),
                      reads=[tP[7]], writes=[t_oab])
            deferred.append((step + 3, ("o", n), tr_out))

        tiles = []

        def add_C(n):
            i, g = n // 2, n % 2
            nch = 1 if i < 4 else 2
            off = i % 4
            nr = 32 * off + 32
            for ch in range(nch):
                last = ch == nch - 1
                nrows = 128
                tiles.append(dict(kind="C", n=n, nrows=nrows, lhsT=kcmpT[:, g, ch * 128:ch * 128 + nrows],
                                  table=(TC[:nrows, g, off] if last else None), first=(ch == 0), last=last,
                                  rhs=RC[:nrows, ch, g, :], acc=acc_view(2, 193), accb=(2, 3)))

        def add_W(n):
            i, g = n // 2, n % 2
            qb = 4 * i + 3
            kbs = [kb for kb in range(qb - 4, qb + 1) if kb >= 0]
            for kb in kbs:
                if kb == qb:
                    table = TW[:, g, 0]
                elif kb == qb - 1:
                    table = TW[:, g, 1]
                elif kb == qb - 4:
                    table = mkf.unsqueeze(1).to_broadcast([128, 4, 128])
                else:
                    table = None
                tiles.append(dict(kind="W", n=n, nrows=128, lhsT=kwT[:, g, kb * 128:(kb + 1) * 128], table=table,
                                  first=(kb == kbs[0]), last=(kb == qb), rhs=vwA[:, kb, g, :], acc=acc_view(4, 129), accb=(4, 5)))

        def add_S(n):
            i, g = n // 2, n % 2
            qb = 4 * i + 3
            for kb in range(qb + 1):
                if kb == qb:
                    table = TW[:, g, 0]
                elif kb == qb - 1:
                    table = TW[:, g, 1]
                else:
                    table = None
                tiles.append(dict(kind="S", n=n, nrows=128, lhsT=ksT[:, g, kb * 128:(kb + 1) * 128], table=table,
                                  first=(kb == 0), last=(kb == qb), rhs=vsA[:, kb, g, :], acc=acc_view(2, 129), accb=(2, 3), kb=kb))
        add_C(0)
        add_W(0)
        for n in range(NIG - 1):
            add_C(n + 1)
            add_W(n + 1)
            add_S(n)
        add_S(NIG - 1)
        for idx, t in enumerate(tiles):
            t["idx"] = idx

        def c_s0(idx):
            t = tiles[idx]
            sb_ = SB[idx % 3]
            nrows = t["nrows"]
            sps = bank(sb_).rearrange("p (h t) -> p h t", h=4)
            qsl = qslice(t["n"])
            if t["kind"] == "S":
                pp = t["n"] % 2
                kb = t["kb"]
                if t["first"]:
                    flush(("sel", t["n"]))

                def mms(e):
                    e.matmul(sps, lhsT=t["lhsT"], rhs=qsl, start=True, stop=False)
                    return e.matmul(sps, lhsT=Eoh[:, kb, :], rhs=nselT[pp].unsqueeze(1).to_broadcast([128, 4, 128]),
                                    start=False, stop=True)
                P.add("pe", mms, reads=[t_nsel[pp]], writes=[tP[sb_]])
            else:
                P.add("pe", lambda e: e.matmul(sps[:nrows], lhsT=t["lhsT"], rhs=qsl, start=True, stop=True),
                      reads=[], writes=[tP[sb_]])

        def c_s1(idx):
            t = tiles[idx]
            sb_ = SB[idx % 3]
            nrows = t["nrows"]
            sps = bank(sb_).rearrange("p (h t) -> p h t", h=4)
            dst, t_dst = pbuf[idx % 4], t_p[idx % 4]
            if t["table"] is None:
                P.add("act", lambda e: e.activation(out=dst[:nrows], in_=sps[:nrows], func=AF.Exp, scale=SCALE),
                      reads=[tP[sb_]], writes=[t_dst])
            else:
                ei = ecnt[0] % 2
                ecnt[0] += 1
                eb = Ebuf[ei].rearrange("p (h t) -> p h t", h=4)
                P.add("dve", lambda e: e.scalar_tensor_tensor(out=eb[:nrows], in0=sps[:nrows], scalar=SCALE, in1=t["table"],
                                                              op0=ALU.mult, op1=ALU.add),
                      reads=[tP[sb_], t_tab], writes=[t_E[ei]])
                P.add("act", lambda e: e.activation(out=dst[:nrows], in_=eb[:nrows], func=AF.Exp),
                      reads=[t_E[ei]], writes=[t_dst])

        def c_s3(idx, step):
            t = tiles[idx]
            nrows = t["nrows"]
            src, t_src = pbuf[idx % 4], t_p[idx % 4]
            acc = t["acc"]

            def pv(e):
                for h in range(4):
                    ins = e.matmul(acc[h], lhsT=src[:nrows, h, :], rhs=t["rhs"],
                                   start=(t["first"] and h % 2 == 0), stop=(t["last"] and h % 2 == 1))
                return ins
            P.add("pe", pv, reads=[t_src], writes=[tP[t["accb"][0]], tP[t["accb"][1]]])
            if t["last"]:
                if t["kind"] == "C":
                    post_C(t["n"], step)
                elif t["kind"] == "W":
                    post_W(t["n"])
                else:
                    post_S(t["n"], step)

        NT = len(tiles)
        for step in range(NT + 3):
            for d in [d for d in deferred if d[0] <= step]:
                d[2]()
                deferred.remove(d)
            if 0 <= step - 3 < NT:
                c_s3(step - 3, step)
            if 0 <= step - 1 < NT:
                c_s1(step - 1)
            if step < NT:
                c_s0(step)
        for d in list(deferred):
            d[2]()
        deferred[:] = []
        dbg_dump("oaT", oaT, [128, 8, 1024], BF16)
        P.barrier()

        if stop_after == "C":
            P.cut = True
        A.seek(35072, 159744)
        h2T = A.alloc([16, 1024], BF16)
        mix = A.alloc([NOWN, D], F32)
        gB = A.alloc([D], F32)
        wob = [wob0, A.alloc([16, 512], BF16)]
        t_mix = P.toks(NOWN, "mix")
        t_g = P.tok("g")
        P.add("sp", lambda e: e.dma_start(out=gA, in_=gains_d[1]), writes=[t_g], dma=True)
        P.add("sp", lambda e: e.dma_start(out=gB, in_=gains_d[2]), writes=[t_g], dma=True)

        def load_wo(n):
            for q4 in range(4):
                P.add("pool", lambda e, n=n, q4=q4: e.dma_start(
                    out=wob[n % 2][:, q4 * 4:(q4 + 1) * 4, :],
                    in_=wo_d[q4 * 512:(q4 + 1) * 512, n * 512:(n + 1) * 512].rearrange("(k p) n -> p k n", p=128)),
                    writes=[t_wob[n % 2]], dma=True)
        load_wo(1)
        pbi = 0
        for n in range(4):
            for blk in range(NOWN):
                pb = pbi % 8
                pbi += 1

                def mmo(e, n=n, blk=blk, pb=pb):
                    for k in range(16):
                        lt = oaT[:, k, blk * 128:(blk + 1) * 128] if k < 8 else ocT[:, k - 8, blk * 128:(blk + 1) * 128]
                        ins = e.matmul(bank(pb), lhsT=lt, rhs=wob[n % 2][:, k, :], start=(k == 0), stop=(k == 15))
                    return ins
                P.add("pe", mmo, reads=[t_wob[n % 2], t_oaoc], writes=[tP[pb]])
                dv = mix[:, blk, n * 512:(n + 1) * 512]
                if pbi % 2 == 0:
                    P.add("act", lambda e, dv=dv, pb=pb: e.copy(out=dv, in_=bank(pb)), reads=[tP[pb]], writes=[t_mix[blk], tP[pb]])
                else:
                    P.add("dve", lambda e, dv=dv, pb=pb: e.tensor_copy(out=dv, in_=bank(pb)), reads=[tP[pb]],
                          writes=[t_mix[blk], tP[pb]])
            if n + 2 < 4:
                load_wo(n + 2)
        A.seek(67840)
        facc = A.alloc([NOWN, D], F32)
        wub1 = A.alloc([16, 512], BF16)
        wdb1 = A.alloc([4, D], BF16)
        junkF = A.alloc([D], BF16)
        A.seek(178176)
        wub = [A.alloc([16, 512], BF16), wub1]
        wdb = [A.alloc([4, D], BF16), wdb1]
        A.seek(OFF_XIN, OFF_XN)
        rbuf = [A.alloc([512], F32) for _ in range(2)]
        aT0 = A.alloc([4, 1024], BF16)
        A.seek(OFF_XN, 35072)
        aT = [aT0, A.alloc([4, 1024], BF16)]
        t_wub = P.toks(2, "wub")
        t_wdb = P.toks(2, "wdb")
        t_aT = P.toks(2, "aT")
        t_rb = P.toks(2, "rb")
        t_facc = P.toks(NOWN, "facc")
        NG = DFF // 512

        def load_wu(G, extra=()):
            for q4 in range(4):
                P.add("pool", lambda e, G=G, q4=q4: e.dma_start(
                    out=wub[G % 2][:, q4 * 4:(q4 + 1) * 4, :],
                    in_=wup_d[q4 * 512:(q4 + 1) * 512, G * 512:(G + 1) * 512].rearrange("(k p) n -> p k n", p=128)),
                    writes=[t_wub[G % 2]] + list(extra), dma=True)

        def load_wd(G, extra=()):
            for q4 in range(4):
                P.add("pool", lambda e, G=G, q4=q4: e.dma_start(
                    out=wdb[G % 2][:, q4:q4 + 1, :],
                    in_=wdn_d[G * 512 + q4 * 128:G * 512 + (q4 + 1) * 128, :].rearrange("(k p) n -> p k n", p=128)),
                    writes=[t_wdb[G % 2]] + list(extra), dma=True)
        load_wu(0, extra=[t_oaoc])
        load_wd(0, extra=[t_oaoc])
        t_h2T = P.tok("h2T")
        t_x1s = P.toks(NOWN, "x1s")
        junkD = wob[0][:, 0:4, :].rearrange("p a b -> p (a b)")
        t_junkD = t_wob[0]
        dstate = {}

        def d_s0(blk):
            xi = blk % 2
            P.add("sp", lambda e: e.dma_start(out=xin[xi], in_=xo[blk * 128:(blk + 1) * 128, :]),
                  writes=[t_xin[xi]], dma=True)

        def d_s1(blk):
            xi = blk % 2
            mb = mix[:, blk, :]
            c, t_c = dstate.pop(blk)
            rms_fin(c, t_c)
            P.add("dve", lambda e: e.scalar_tensor_tensor(out=mb, in0=mb, scalar=c[:, 3:4], in1=gA,
                                                          op0=ALU.mult, op1=ALU.mult),
                  reads=[t_c, t_g], writes=[t_mix[blk]])
            P.add("dve", lambda e: e.tensor_tensor(out=mb, in0=mb, in1=xin[xi], op=ALU.add),
                  reads=[t_xin[xi]], writes=[t_mix[blk]])
            P.add("pool", lambda e: e.dma_start(out=x1s[blk * 128:(blk + 1) * 128, :], in_=mb),
                  reads=[t_mix[blk]], writes=[t_x1s[blk]], dma=True)
            dstate[("b", blk)] = rms_sq(mb, t_mix[blk], junkD, t_junkD)

        def d_s2(blk):
            xi = blk % 2
            mb = mix[:, blk, :]
            c2, t_c2 = dstate.pop(("b", blk))
            rms_fin(c2, t_c2)
            P.add("dve", lambda e: e.scalar_tensor_tensor(out=xn[xi], in0=mb, scalar=c2[:, 3:4], in1=gB,
                                                          op0=ALU.mult, op1=ALU.mult),
                  reads=[t_c2, t_mix[blk], t_g], writes=[t_xn[xi]])

        def d_s3(blk):
            xi = blk % 2
            transpose_block(xi, 2 * xi, h2T[:, :, blk * 128:(blk + 1) * 128], t_h2T, "act")
        for blk in range(NOWN):
            dstate[blk] = rms_sq(mix[:, blk, :], t_mix[blk], junkD, t_junkD)
        pipeline(NOWN, [d_s0, d_s1, d_s2, d_s3])
        P.barrier()

        if stop_after == "D":
            P.cut = True
        ucnt = [0]
        dcnt = [0]

        def up(G):
            for fc in range(4):
                for half in range(2):
                    pb = ucnt[0] % 4
                    ri = ucnt[0] % 2
                    ucnt[0] += 1

                    def mmu(e, G=G, fc=fc, half=half, pb=pb):
                        for k in range(16):
                            ins = e.matmul(bank(pb), lhsT=wub[G % 2][:, k, fc * 128:(fc + 1) * 128],
                                           rhs=h2T[:, k, half * 512:(half + 1) * 512], start=(k == 0), stop=(k == 15))
                        return ins
                    P.add("pe", mmu, reads=[t_wub[G % 2]], writes=[tP[pb]])
                    P.add("act", lambda e, pb=pb, ri=ri: e.activation(out=rbuf[ri], in_=bank(pb), func=AF.Relu),
                          reads=[tP[pb]], writes=[t_rb[ri], tP[pb]])
                    P.add("pool", lambda e, G=G, fc=fc, half=half, ri=ri: e.tensor_tensor(
                        out=aT[G % 2][:, fc, half * 512:(half + 1) * 512], in0=rbuf[ri], in1=rbuf[ri], op=ALU.mult),
                        reads=[t_rb[ri]], writes=[t_aT[G % 2]])

        def down(G):
            for blk in range(NOWN):
                for n in range(4):
                    pb = 4 + dcnt[0] % 4
                    dcnt[0] += 1

                    def mmd(e, G=G, blk=blk, n=n, pb=pb):
                        for fc in range(4):
                            ins = e.matmul(bank(pb), lhsT=aT[G % 2][:, fc, blk * 128:(blk + 1) * 128],
                                           rhs=wdb[G % 2][:, fc, n * 512:(n + 1) * 512], start=(fc == 0), stop=(fc == 3))
                        return ins
                    P.add("pe", mmd, reads=[t_aT[G % 2], t_wdb[G % 2]], writes=[tP[pb]])
                    fv = facc[:, blk, n * 512:(n + 1) * 512]
                    if G == 0:
                        P.add("act", lambda e, fv=fv, pb=pb: e.copy(out=fv, in_=bank(pb)), reads=[tP[pb]],
                              writes=[t_facc[blk], tP[pb]])
                    else:
                        P.add("dve", lambda e, fv=fv, pb=pb: e.tensor_tensor(out=fv, in0=bank(pb), in1=fv, op=ALU.add),
                              reads=[tP[pb]], writes=[t_facc[blk], tP[pb]])
                if G == NG - 1:
                    if blk >= 1:
                        f_s1(blk - 1)
                    f_s0(blk)
            if G == NG - 1:
                f_s1(NOWN - 1)
        t_g2 = P.tok("g2")
        P.add("sp", lambda e: e.dma_start(out=gA, in_=gains_d[3]), writes=[t_g2], dma=True)
        t_out = P.tok("out")
        t_junkF = P.tok("junkF")
        xinF = [wub[i][:, 0:8, :].rearrange("p a b -> p (a b)").bitcast(F32) for i in range(2)]
        fstate = {}

        def f_s0(blk):
            xi = blk % 2
            fb = facc[:, blk, :]
            P.add("sp", lambda e: e.dma_start(out=xinF[xi], in_=x1s[blk * 128:(blk + 1) * 128, :]),
                  reads=[t_x1s[blk]], writes=[t_wub[xi]], dma=True)
            fstate[blk] = rms_sq(fb, t_facc[blk], junkF, t_junkF)

        def f_s1(blk):
            xi = blk % 2
            fb = facc[:, blk, :]
            c, t_c = fstate.pop(blk)
            rms_fin(c, t_c)
            P.add("dve", lambda e: e.scalar_tensor_tensor(out=fb, in0=fb, scalar=c[:, 3:4], in1=gA,
                                                          op0=ALU.mult, op1=ALU.mult),
                  reads=[t_c, t_g2], writes=[t_facc[blk]])
            P.add("dve", lambda e: e.tensor_tensor(out=fb, in0=fb, in1=xinF[xi], op=ALU.add),
                  reads=[t_wub[xi]], writes=[t_facc[blk]])
            P.add("pool", lambda e: e.dma_start(out=out_d[blk * 128:(blk + 1) * 128, :], in_=fb),
                  reads=[t_facc[blk]], writes=[t_out], dma=True)

        load_wu(1)
        load_wd(1)
        up(0)
        for G in range(NG):
            if G + 1 < NG:
                up(G + 1)
            if G + 2 < NG:
                load_wu(G + 2)
            down(G)
            if G + 2 < NG:
                load_wd(G + 2)

        P.cut = False
        P.barrier()
        P.emit(block, csem, dsem)
    return nc, dbg_outs


def _rel_bucket(dist):
    n = np.maximum(dist, 0)
    nf = np.maximum(n, 1).astype(np.float32)
    large = 16 + (np.log(nf / np.float32(16)) / np.float32(math.log(128 / 16)) * np.float32(16)).astype(np.int32)
    large = np.minimum(large, 31)
    return np.where(n < 16, n, large)


def _host_inputs(inputs):
    x = np.ascontiguousarray(np.asarray(inputs["x"], dtype=np.float32))
    w_in = np.asarray(inputs["w_in"], dtype=np.float32)[0]
    rel_bias = np.asarray(inputs["rel_bias"], dtype=np.float32)
    wq = np.ascontiguousarray(w_in[:, 0:1024])
    o = 1024
    kc, vc, ks, vs, kw, vw = [w_in[:, o + 256 * n: o + 256 * (n + 1)] for n in range(6)]
    wkv = np.ascontiguousarray(np.concatenate([kc, vc, ks, kw, vs, vw], axis=1))
    wg = np.ascontiguousarray(w_in[:, 2560:2584])
    wc = w_in[:, 2584:5656].reshape(2048, 3, 8, 128).transpose(0, 2, 1, 3)
    wcv = np.ascontiguousarray(wc.reshape(2048, 8, 384))
    common = {
        "wq": wq, "wkv": wkv, "wg": wg, "wcv": wcv,
        "w1k": np.ascontiguousarray(inputs["w_cmp_k1"][0], dtype=np.float32),
        "w1v": np.ascontiguousarray(inputs["w_cmp_v1"][0], dtype=np.float32),
        "w2k": np.ascontiguousarray(inputs["w_cmp_k2"][0], dtype=np.float32),
        "w2v": np.ascontiguousarray(inputs["w_cmp_v2"][0], dtype=np.float32),
        "peT": np.ascontiguousarray(np.asarray(inputs["pe_cmp"], dtype=np.float32)[0].T),
        "convw": np.ascontiguousarray(np.asarray(inputs["conv_w"], dtype=np.float32)[0].reshape(3, 8, 128).transpose(2, 1, 0)),
        "wo": np.ascontiguousarray(inputs["w_o"][0], dtype=np.float32),
        "wup": np.ascontiguousarray(inputs["w_up"][0], dtype=np.float32),
        "wdn": np.ascontiguousarray(inputs["w_down"][0], dtype=np.float32),
        "ident": np.eye(128, dtype=np.float32),
    }
    gains = np.stack([np.asarray(inputs[k], dtype=np.float32)[0] for k in ("g_pre_mix", "g_post_mix", "g_pre_ffn", "g_post_ffn")])
    common["gains"] = np.ascontiguousarray(np.broadcast_to(gains[:, None, :], (4, 128, 2048)))
    kl = np.arange(128)[:, None]
    tl = np.arange(128)[None, :]
    bw = np.zeros((128, 2, 2, 4, 128), np.float32)
    for g in range(2):
        for tbl in range(2):
            bk = _rel_bucket(tl - kl + 128 * tbl)
            for h in range(4):
                bw[:, g, tbl, h, :] = rel_bias[4 * g + h][bk]
    common["bwraw"] = bw
    bc = np.zeros((128, 2, 4, 4, 128), np.float32)
    mkc = np.zeros((128, 4, 128), np.float32)
    r = np.arange(128)[:, None]
    for off in range(4):
        dist = 512 * off + 353 + tl - 16 * r
        bk = _rel_bucket(dist)
        mkc[:, off, :] = np.where(dist >= 0, 0.0, NEGM)
        for g in range(2):
            for h in range(4):
                bc[:, g, off, h, :] = rel_bias[4 * g + h][bk]
    common["bcraw"] = bc
    common["mkc"] = mkc
    common["b31"] = np.ascontiguousarray(np.broadcast_to(rel_bias[:, 31][None, :], (128, 8)))
    eoh = np.zeros((64, NB, 128), np.float32)
    for kb_ in range(NB):
        eoh[2 * kb_, kb_, 0:64] = 1.0
        eoh[2 * kb_ + 1, kb_, 64:128] = 1.0
    common["eoh"] = eoh.reshape(64, NB * 128)
    common["mkdiag"] = np.where(kl <= tl, 0.0, NEGM).astype(np.float32)
    common["mkfar"] = np.where(kl > tl, 0.0, NEGM).astype(np.float32)
    in_maps = []
    for c in range(8):
        b, j = c // 4, c % 4
        Pd = (3 - j) * 128
        m = dict(common)
        xbl = np.zeros((T, D), np.float32)
        xbl[Pd:] = x[b, :T - Pd]
        m["xb"] = xbl
        own = np.concatenate([np.arange((4 * i + j) * 128, (4 * i + j + 1) * 128) for i in range(NOWN)])
        m["xo"] = np.ascontiguousarray(x[b, own])
        xhl = np.zeros((16, D), np.float32)
        for i in range(NOWN):
            t0 = (4 * i + j) * 128
            if t0 >= 2:
                xhl[2 * i:2 * i + 2] = x[b, t0 - 2:t0]
        m["xh"] = xhl
        tloc = np.arange(128)[:, None] + 128 * np.arange(NB)[None, :]
        m["validk"] = (tloc >= Pd).astype(np.float32)
        cl = np.arange(128)[:, None] + 128 * np.arange(2)[None, :]
        validc = ((cl >= Pd // 16) & (cl <= 254)).astype(np.float32)
        jl = np.arange(64)[None, None, :]
        ci = (16 * cl)[:, :, None]
        ov = ((ci < 64 * jl + 64) & (ci + 32 > 64 * jl)).astype(np.float32)
        rcx = np.zeros((128, 2, 65), np.float32)
        rcx[:, :, 0] = validc
        rcx[:, :, 1:] = ov * validc[:, :, None]
        m["rcx"] = rcx
        imax = np.zeros((128, NOWN, 64), np.float32)
        imin = np.zeros((128, NOWN, 64), np.float32)
        first = Pd // 64
        for i in range(NOWN):
            t = (4 * i + 3) * 128 + np.arange(128)[:, None]
            cur = t // 64
            jj = np.arange(64)[None, :]
            mx = np.zeros((128, 64), np.float32)
            mx = np.where(jj == cur - 1, 1e9, mx)
            mx = np.where(jj == cur, 2e9, mx)
            mx = np.where(jj == first, 3e9, mx)
            imax[:, i, :] = mx
            imin[:, i, :] = np.where((jj * 64 <= t) & (jj >= first), 3.0e38, -1.0e30)
        m["impmax"] = imax
        m["impmin"] = imin
        in_maps.append(m)
    return in_maps


_CACHE = {}


def kernel(**inputs):
    in_maps = _host_inputs(inputs)
    if "nc" not in _CACHE:
        _CACHE["nc"] = build(False)[0]
    nc = _CACHE["nc"]
    res = run_bass_kernel_spmd(nc, in_maps, core_ids=list(range(8)))
    out = np.zeros((2, T, D), np.float32)
    for c in range(8):
        b, j = c // 4, c % 4
        o = np.asarray(res.results[c]["out"], dtype=np.float32)
        for i in range(NOWN):
            t0 = (4 * i + j) * 128
            out[b, t0:t0 + 128] = o[i * 128:(i + 1) * 128]
    return out
```

```python
import math
from contextlib import ExitStack

import numpy as np
import concourse.bass as bass
import concourse.mybir as mybir
from concourse.bass_utils import run_bass_kernel_spmd

F32 = mybir.dt.float32
BF16 = mybir.dt.bfloat16
AF = mybir.ActivationFunctionType
ALU = mybir.AluOpType

ENGS = ("pe", "act", "dve", "pool", "sp")

D = 2048
T = 4096
NB = 32
NOWN = 8
DFF = 8192
EPS = 1e-6
SCALE = 128 ** -0.5
NEGM = -30000.0


class Tok:
    __slots__ = ("name", "w", "r", "excl")

    def __init__(self, name="", excl=False):
        self.name = name
        self.w = None
        self.r = []
        self.excl = excl


class Op:
    __slots__ = ("eng", "fn", "seq", "waits", "ms", "msval", "dma", "slot", "val")


class Prog:
    def __init__(self, n_dma_sems=32):
        self.ops = {e: [] for e in ENGS}
        self.seen = {e: {} for e in ENGS}
        self.NS = n_dma_sems
        self.H = n_dma_sems // 2
        self.ndma = {"sp": 0, "pool": 0}
        self.dma_last = [None] * n_dma_sems
        self.cut = False

    def tok(self, name=""):
        return Tok(name)

    def toks(self, n, name=""):
        return [Tok(f"{name}{i}") for i in range(n)]

    def add(self, eng, fn, reads=(), writes=(), dma=False, extra_deps=()):
        if self.cut:
            return None
        ex = [t for t in reads if t.excl]
        if ex:
            reads = [t for t in reads if not t.excl]
            writes = list(writes) + [t for t in ex if t not in writes]
        op = Op()
        op.eng = eng
        op.fn = fn
        op.dma = dma
        op.ms = False
        op.msval = None
        op.seq = len(self.ops[eng])
        deps = list(extra_deps)
        for t in reads:
            if t.w is not None:
                deps.append(t.w)
        for t in writes:
            if t.w is not None:
                deps.append(t.w)
            deps.extend(t.r)
        if dma:
            assert eng in ("sp", "pool")
            nd = self.ndma[eng]
            slot = nd % self.H + (0 if eng == "sp" else self.H)
            op.slot = slot
            op.val = 16 * (nd // self.H + 1)
            if self.dma_last[slot] is not None:
                deps.append(self.dma_last[slot])
            self.dma_last[slot] = op
            self.ndma[eng] = nd + 1
        waits = []
        seen = self.seen[eng]
        for d in deps:
            if d is op:
                continue
            if d.dma:
                key = ("d", d.slot)
                lvl = d.val
            else:
                if d.eng == eng and eng == "pe":
                    continue
                key = ("c", d.eng)
                lvl = d.seq
            if seen.get(key, -1) >= lvl:
                continue
            seen[key] = lvl
            waits.append(d)
            if not d.dma:
                d.ms = True
        op.waits = waits
        for t in reads:
            t.r.append(op)
        for t in writes:
            t.w = op
            t.r = []
        self.ops[eng].append(op)
        return op

    def barrier(self):
        lasts = []
        for e in ENGS:
            for o in reversed(self.ops[e]):
                if o.fn is not None and not o.dma:
                    lasts.append(o)
                    break
        dmas = [d for d in self.dma_last if d is not None]
        for e in ENGS:
            self.add(e, None, extra_deps=lasts + dmas)

    def emit(self, block, csem, dsem):
        for e in ENGS:
            c = 0
            for op in self.ops[e]:
                if op.ms:
                    c += 1
                    op.msval = c

        def run(ename):
            def body(eng):
                for op in self.ops[ename]:
                    for d in op.waits:
                        if d.dma:
                            eng.wait_ge(dsem[d.slot], d.val)
                        else:
                            eng.wait_ge(csem[d.eng], d.msval)
                    ins = op.fn(eng) if op.fn is not None else None
                    if op.dma:
                        ins.then_inc(dsem[op.slot], 16)
                    elif op.ms:
                        assert ins is not None
                        ins.then_inc(csem[ename], 1)
            return body

        block.tensor(run("pe"))
        block.scalar(run("act"))
        block.vector(run("dve"))
        block.gpsimd(run("pool"))
        block.sync(run("sp"))


class Arena:
    def __init__(self, ap, nbytes):
        self.ap = ap
        self.n = nbytes
        self.top = 0

    def seek(self, off, limit=None):
        assert off % 64 == 0
        self.top = off
        self.limit = limit if limit is not None else self.n

    def alloc(self, shape, dtype):
        esz = 2 if dtype == BF16 else 4
        n = 1
        for s in shape:
            n *= s
        off = (self.top + 63) // 64 * 64
        nb = n * esz
        self.top = off + nb
        assert self.top <= getattr(self, "limit", self.n), f"arena overflow {self.top} > {getattr(self, 'limit', self.n)}"
        v = self.ap[:, off // 2: off // 2 + nb // 2]
        if dtype == F32:
            v = v.bitcast(F32)
        if len(shape) == 2:
            v = v.rearrange("p (a b) -> p a b", a=shape[0])
        elif len(shape) == 3:
            v = v.rearrange("p (a b c) -> p a b c", a=shape[0], b=shape[1])
        elif len(shape) == 4:
            v = v.rearrange("p (a b c d) -> p a b c d", a=shape[0], b=shape[1], c=shape[2])
        return v


def build(dbg=False, stop_after=None):
    nc = bass.Bass("TRN2", target_bir_lowering=False)

    def din(name, shape):
        return nc.dram_tensor(name, list(shape), F32, kind="ExternalInput").ap()

    xb = din("xb", [T, D])
    xo = din("xo", [1024, D])
    xh = din("xh", [16, D])
    validk_d = din("validk", [128, NB])
    rcx_d = din("rcx", [128, 2, 65])
    impmax_d = din("impmax", [128, NOWN, 64])
    impmin_d = din("impmin", [128, NOWN, 64])
    bw_d = din("bwraw", [128, 2, 2, 4, 128])
    bc_d = din("bcraw", [128, 2, 4, 4, 128])
    b31_d = din("b31", [128, 8])
    mkd_d = din("mkdiag", [128, 128])
    mkf_d = din("mkfar", [128, 128])
    mkc_d = din("mkc", [128, 4, 128])
    ident_d = din("ident", [128, 128])
    eoh_d = din("eoh", [64, NB * 128])
    gains_d = din("gains", [4, 128, D])
    wq_d = din("wq", [D, 1024])
    wkv_d = din("wkv", [D, 1536])
    wg_d = din("wg", [D, 24])
    wcv_d = din("wcv", [D, 8, 384])
    w1k_d = din("w1k", [32, 128, 256])
    w1v_d = din("w1v", [32, 128, 256])
    w2k_d = din("w2k", [256, 128])
    w2v_d = din("w2v", [256, 128])
    peT_d = din("peT", [128, 32])
    convw_d = din("convw", [128, 8, 3])
    wo_d = din("wo", [D, D])
    wup_d = din("wup", [D, DFF])
    wdn_d = din("wdn", [DFF, D])
    out_d = nc.dram_tensor("out", [1024, D], F32, kind="ExternalOutput").ap()
    x1s = nc.dram_tensor("x1s", [1024, D], F32).ap()
    dbg_outs = {}

    P = Prog()
    ARENA_BYTES = 206 * 1024
    with ExitStack() as es:
        E = es.enter_context
        arena_t = E(nc.sbuf_tensor("arena", [128, ARENA_BYTES // 2], BF16))
        ps_t = E(nc.psum_tensor("ps", [128, 4096], F32))
        csem = {e: E(nc.semaphore("c_" + e)) for e in ENGS}
        dsem = [E(nc.semaphore(f"d{i}")) for i in range(P.NS)]
        block = E(nc.Block())

        A = Arena(arena_t[:], ARENA_BYTES)
        PS = ps_t[:]
        PSB = PS.bitcast(BF16)

        def bank(b, n=512):
            return PS[:, b * 512: b * 512 + n]

        def bankb(b, n=1024):
            return PSB[:, b * 1024: b * 1024 + n]

        tP = [Tok(f"ps{i}", excl=True) for i in range(8)]

        def dbg_dump(name, ap, shape, dtype=F32):
            if not dbg:
                return
            P.barrier()
            o = nc.dram_tensor("dbg_" + name, list(shape), dtype, kind="ExternalOutput").ap()
            dbg_outs[name] = o
            P.add("sp", lambda e: e.dma_start(out=o, in_=ap), dma=True)

        ident = A.alloc([128], BF16)
        scr = A.alloc([512], F32)
        gA = A.alloc([D], F32)
        OFF_XIN = A.top
        xin = [A.alloc([D], F32) for _ in range(2)]
        OFF_XN = A.top
        xn = [A.alloc([D], BF16) for _ in range(2)]
        assert A.top == 35072, A.top
        t_xin = P.toks(2, "xin")
        t_xn = P.toks(2, "xn")
        ksT = A.alloc([2, T], BF16)
        kwT = A.alloc([2, T], BF16)
        vsA = A.alloc([NB, 2, 129], BF16)
        vwA = A.alloc([NB, 2, 129], BF16)
        kcmpT = A.alloc([2, 256], BF16)
        RC = A.alloc([2, 2, 193], BF16)
        assert A.top <= 103488, A.top
        A.seek(103488)
        kcT = A.alloc([2, 16, 256], BF16)
        vcT = A.alloc([2, 16, 256], BF16)
        assert A.top == 136256

        scr_i = [0]

        def scol(n=1):
            c = scr_i[0]
            scr_i[0] += n
            assert scr_i[0] <= 512
            return scr[:, c:c + n]

        t_set = P.tok("setup")
        tmpf = xin[1]
        P.add("sp", lambda e: e.dma_start(out=tmpf[:, 0:128], in_=ident_d), writes=[t_set], dma=True)
        P.add("dve", lambda e: e.tensor_copy(out=ident, in_=tmpf[:, 0:128]), reads=[t_set], writes=[t_set])
        P.add("sp", lambda e: e.dma_start(out=tmpf[:, 128:160], in_=validk_d), writes=[t_set], dma=True)
        for tens in (vsA, vwA):
            for g in range(2):
                P.add("dve", lambda e, tens=tens, g=g: e.tensor_copy(out=tens[:, :, g, 128], in_=tmpf[:, 128:160]),
                      reads=[t_set], writes=[t_set])
        P.add("sp", lambda e: e.dma_start(out=gA, in_=gains_d[0]), writes=[t_set], dma=True)
        P.add("pool", lambda e: e.memset(kcmpT, 0.0), writes=[t_set])
        P.add("pool", lambda e: e.memset(RC, 0.0), writes=[t_set])
        P.barrier()

        def rms_sq(src, t_src, junk, t_junk, np_=128):
            c = scol(4)
            t_c = P.tok("c")
            P.add("act", lambda e: e.activation(out=junk[:np_], in_=src, func=AF.Square, accum_out=c[:np_, 0:1]),
                  reads=[t_src], writes=[t_junk, t_c])
            return c, t_c

        def rms_fin(c, t_c, np_=128):
            P.add("dve", lambda e: e.tensor_scalar(out=c[:np_, 1:2], in0=c[:np_, 0:1], scalar1=1.0 / D, scalar2=EPS,
                                                   op0=ALU.mult, op1=ALU.add), reads=[t_c], writes=[t_c])
            P.add("act", lambda e: e.activation(out=c[:np_, 2:3], in_=c[:np_, 1:2], func=AF.Sqrt), reads=[t_c], writes=[t_c])
            P.add("dve", lambda e: e.reciprocal(out=c[:np_, 3:4], in_=c[:np_, 2:3]), reads=[t_c], writes=[t_c])

        def rms_to_xn(src, t_src, gt, xnb, t_xnb, np_=128):
            c, t_c = rms_sq(src, t_src, xnb, t_xnb, np_)
            rms_fin(c, t_c, np_)
            return c, t_c

        def pipeline(n, stages):
            K_ = len(stages)
            for step in range(n + K_ - 1):
                for k in range(K_ - 1, -1, -1):
                    it = step - k
                    if 0 <= it < n:
                        stages[k](it)

        nstate = {}

        def norm_s0(key, src_dram, xi, n_rows=128):
            xin_b, xn_b = xin[xi], xn[xi]
            P.add("sp", lambda e: e.dma_start(out=xin_b[:n_rows], in_=src_dram), writes=[t_xin[xi]], dma=True)
            nstate[key] = rms_sq(xin_b[:n_rows], t_xin[xi], xn_b, t_xn[xi], n_rows)

        def norm_s1(key, gt, xi, n_rows=128):
            xin_b, xn_b = xin[xi], xn[xi]
            c, t_c = nstate.pop(key)
            rms_fin(c, t_c, n_rows)
            P.add("dve", lambda e: e.scalar_tensor_tensor(out=xn_b[:n_rows], in0=xin_b[:n_rows], scalar=c[:n_rows, 3:4],
                                                          in1=gt[:n_rows], op0=ALU.mult, op1=ALU.mult),
                  reads=[t_xin[xi], t_c], writes=[t_xn[xi]])

        def norm_block(src_dram, gt, xi, n_rows=128):
            norm_s0(("nb", xi), src_dram, xi, n_rows)
            norm_s1(("nb", xi), gt, xi, n_rows)

        def transpose_block(xi, pb, dst, t_dst, evac, n_rows=128):
            xn_b = xn[xi]
            pv = PSB[:, pb * 1024: pb * 1024 + 16 * n_rows].rearrange("p (k t) -> p k t", k=16)

            def tr(e):
                for k in range(16):
                    ins = e.transpose(out=pv[:, k, :], in_=xn_b[:n_rows, k * 128:(k + 1) * 128],
                                      identity=ident[:n_rows, :n_rows])
                return ins
            P.add("pe", tr, reads=[t_xn[xi]], writes=[tP[pb], tP[pb + 1]])
            if evac == "act":
                P.add("act", lambda e: e.copy(out=dst, in_=pv), reads=[tP[pb], tP[pb + 1]], writes=[t_dst])
            else:
                P.add("dve", lambda e: e.tensor_copy(out=dst, in_=pv), reads=[tP[pb], tP[pb + 1]], writes=[t_dst])

        wkv = A.alloc([16, 1536], BF16)
        hT = [A.alloc([16, 256], BF16) for _ in range(2)]
        t_hT = P.toks(2, "hT")
        t_wkv = P.toks(4, "wkv")
        for cg in range(4):
            P.add("pool", lambda e, cg=cg: e.dma_start(
                out=wkv[:, :, cg * 384:(cg + 1) * 384],
                in_=wkv_d[:, cg * 384:(cg + 1) * 384].rearrange("(k p) n -> p k n", p=128)),
                writes=[t_wkv[cg]], dma=True)
        fdst = [(kcT, 0), (kcT, 1), (vcT, 0), (vcT, 1), (ksT, 0), (ksT, 1), (kwT, 0), (kwT, 1)]
        evc = [0]

        def a1_kv(tt):
            hb = hT[tt % 2]
            for fg in range(8):
                pb = 4 + evc[0] % 4
                evc[0] += 1
                dst, g = fdst[fg]

                def mm(e, fg=fg, hb=hb, pb=pb):
                    for k in range(16):
                        ins = e.matmul(bank(pb, 256), lhsT=wkv[:, k, fg * 128:(fg + 1) * 128], rhs=hb[:, k, :],
                                       start=(k == 0), stop=(k == 15))
                    return ins
                P.add("pe", mm, reads=[t_hT[tt % 2], t_wkv[(fg * 128) // 384]], writes=[tP[pb]])
                if fg < 4:
                    dv = dst[:, g, :, tt * 16:(tt + 1) * 16].rearrange("p r m -> p m r")
                    sv = bank(pb, 256).rearrange("p (m r) -> p m r", r=16)
                else:
                    dv = dst[:, g, tt * 256:(tt + 1) * 256]
                    sv = bank(pb, 256)
                if evc[0] % 2 == 0:
                    P.add("act", lambda e, dv=dv, sv=sv: e.copy(out=dv, in_=sv), reads=[tP[pb]], writes=[])
                else:
                    P.add("dve", lambda e, dv=dv, sv=sv: e.tensor_copy(out=dv, in_=sv), reads=[tP[pb]], writes=[])
            for sb in range(2):
                blk = 2 * tt + sb
                pb = 4 + evc[0] % 4
                evc[0] += 1

                def mm2(e, hb=hb, sb=sb, pb=pb):
                    for k in range(16):
                        ins = e.matmul(bank(pb), lhsT=hb[:, k, sb * 128:(sb + 1) * 128], rhs=wkv[:, k, 1024:1536],
                                       start=(k == 0), stop=(k == 15))
                    return ins
                P.add("pe", mm2, reads=[t_hT[tt % 2], t_wkv[2], t_wkv[3]], writes=[tP[pb]])
                pv = bank(pb).rearrange("p (a g d) -> p a g d", a=2, g=2)
                if blk % 2 == 0:
                    P.add("act", lambda e, blk=blk, pv=pv: e.copy(out=vsA[:, blk, :, 0:128], in_=pv[:, 0]), reads=[tP[pb]], writes=[])
                    P.add("act", lambda e, blk=blk, pv=pv: e.copy(out=vwA[:, blk, :, 0:128], in_=pv[:, 1]), reads=[tP[pb]], writes=[])
                else:
                    P.add("dve", lambda e, blk=blk, pv=pv: e.tensor_copy(out=vsA[:, blk, :, 0:128], in_=pv[:, 0]), reads=[tP[pb]], writes=[])
                    P.add("dve", lambda e, blk=blk, pv=pv: e.tensor_copy(out=vwA[:, blk, :, 0:128], in_=pv[:, 1]), reads=[tP[pb]], writes=[])

        def a1_s0(blk):
            norm_s0(("a1", blk), xb[blk * 128:(blk + 1) * 128, :], blk % 2)

        def a1_s1(blk):
            norm_s1(("a1", blk), gA, blk % 2)

        def a1_s2(blk):
            tt, sb = blk // 2, blk % 2
            transpose_block(blk % 2, 2 * (blk % 2), hT[tt % 2][:, :, sb * 128:(sb + 1) * 128], t_hT[tt % 2],
                            "act" if blk % 2 == 0 else "dve")

        def a1_s3(blk):
            if blk % 2 == 1:
                a1_kv(blk // 2)
        for step in range(NB + 3):
            if 0 <= step - 2 < NB:
                a1_s2(step - 2)
            if 0 <= step - 1 < NB:
                a1_s1(step - 1)
            if step < NB:
                a1_s0(step)
            if 0 <= step - 3 < NB:
                a1_s3(step - 3)
        P.barrier()
        A.seek(136256)

        if stop_after == "A1":
            P.cut = True
        W1 = [A.alloc([32, 256], BF16) for _ in range(2)]
        W2 = [A.alloc([2, 128], BF16) for _ in range(2)]
        peT = A.alloc([32], BF16)
        hidT = [A.alloc([2, 256], BF16) for _ in range(2)]
        cbias = A.alloc([4], F32)
        rcx = A.alloc([2, 65], F32)
        t_a2 = P.tok("a2w")
        t_w1 = P.toks(2, "w1")
        P.add("pool", lambda e: e.dma_start(out=peT, in_=peT_d), writes=[t_a2], dma=True)
        for ty, (w1d, w2d) in enumerate(((w1k_d, w2k_d), (w1v_d, w2v_d))):
            for l4 in range(4):
                P.add("pool", lambda e, ty=ty, w1d=w1d, l4=l4: e.dma_start(
                    out=W1[ty][:, l4 * 8:(l4 + 1) * 8, :], in_=w1d[l4 * 8:(l4 + 1) * 8].rearrange("l d h -> d l h")),
                    writes=[t_w1[ty]], dma=True)
            P.add("pool", lambda e, ty=ty, w2d=w2d: e.dma_start(
                out=W2[ty], in_=w2d.rearrange("(c p) e -> p c e", p=128)), writes=[t_w1[ty]], dma=True)
        P.add("sp", lambda e: e.dma_start(out=rcx, in_=rcx_d), writes=[t_a2], dma=True)
        for g in range(2):
            P.add("dve", lambda e, g=g: e.tensor_copy(out=RC[:, :, g, 128:193], in_=rcx), reads=[t_a2], writes=[])
        t_hid = P.toks(2, "hid")
        t_cb = P.tok("cb")
        pbi = 0
        for ty in range(2):
            src = kcT if ty == 0 else vcT
            pbb = 7

            def mmb(e, ty=ty):
                for hh in range(2):
                    for l in range(32):
                        ins = e.matmul(bank(7)[:, hh:hh + 1], lhsT=W1[ty][:, l, hh * 128:(hh + 1) * 128], rhs=peT[:, l:l + 1],
                                       start=(l == 0), stop=(l == 31))
                return ins
            P.add("pe", mmb, reads=[t_a2, t_w1[ty]], writes=[tP[7]])
            P.add("dve", lambda e, ty=ty: e.tensor_copy(out=cbias[:, 2 * ty:2 * ty + 2], in_=bank(7)[:, 0:2]),
                  reads=[tP[7]], writes=[t_cb])
            for g in range(2):
                hd = hidT[g]
                for hh in range(2):
                    pb = pbi % 4
                    pbi += 1

                    def mmc(e, ty=ty, g=g, hh=hh, pb=pb, src=src):
                        for l in range(32):
                            ins = e.matmul(bank(pb, 255), lhsT=W1[ty][:, l, hh * 128:(hh + 1) * 128],
                                           rhs=src[:, g, l % 16, l // 16:l // 16 + 255], start=(l == 0), stop=(l == 31))
                        return ins
                    P.add("pe", mmc, reads=[t_w1[ty]], writes=[tP[pb]])
                    P.add("act", lambda e, ty=ty, hh=hh, pb=pb, hd=hd: e.activation(
                        out=hd[:, hh, 0:255], in_=bank(pb, 255), func=AF.Silu, bias=cbias[:, 2 * ty + hh:2 * ty + hh + 1]),
                        reads=[tP[pb], t_cb], writes=[t_hid[g]])
                if ty == 0:
                    pb = 4 + g

                    def mmk(e, hd=hd, pb=pb):
                        for hh in range(2):
                            ins = e.matmul(bank(pb, 255), lhsT=W2[0][:, hh, :], rhs=hd[:, hh, 0:255],
                                           start=(hh == 0), stop=(hh == 1))
                        return ins
                    P.add("pe", mmk, reads=[t_hid[g], t_w1[0]], writes=[tP[pb]])
                    P.add("dve", lambda e, g=g, pb=pb: e.tensor_copy(out=kcmpT[:, g, 0:255], in_=bank(pb, 255)),
                          reads=[tP[pb]], writes=[tP[pb]])
                else:
                    for ch in range(2):
                        n = 128 if ch == 0 else 127
                        pb = 4 + (g * 2 + ch) % 2

                        def mmv(e, hd=hd, pb=pb, ch=ch, n=n):
                            for hh in range(2):
                                ins = e.matmul(bank(pb, 128)[:n], lhsT=hd[:, hh, ch * 128:ch * 128 + n], rhs=W2[1][:, hh, :],
                                               start=(hh == 0), stop=(hh == 1))
                            return ins
                        P.add("pe", mmv, reads=[t_hid[g], t_w1[1]], writes=[tP[pb]])
                        P.add("dve", lambda e, g=g, pb=pb, ch=ch, n=n: e.tensor_scalar(
                            out=RC[:n, ch, g, 0:128], in0=bank(pb, 128)[:n], scalar1=rcx[:n, ch, 0:1], scalar2=None,
                            op0=ALU.mult), reads=[tP[pb], t_a2], writes=[tP[pb]])
        dbg_dump("ksT", ksT.bitcast(F32) if False else ksT, [128, 2, T], BF16)
        dbg_dump("kcmpT", kcmpT, [128, 2, 256], BF16)
        dbg_dump("RC", RC, [128, 2, 2, 193], BF16)
        dbg_dump("vsA", vsA, [128, NB, 2, 129], BF16)
        P.barrier()

        if stop_after == "A2":
            P.cut = True
        A.seek(194560)
        ocT = A.alloc([8, 1024], BF16)
        A.seek(178176, 194560)
        oaT = A.alloc([8, 1024], BF16)
        A.seek(178176, 194560)
        wbuf0 = A.alloc([16, 384], BF16)
        A.seek(103488, 178176)
        qT = A.alloc([8, 1024], BF16)
        gates = A.alloc([NOWN, 24], F32)
        convw = A.alloc([8, 3], F32)
        OFF_BC = (A.top + 63) // 64 * 64
        hTo = A.alloc([16, 1024], BF16)
        hTh = A.alloc([16, 16], BF16)
        wbuf = [wbuf0, A.alloc([16, 384], BF16)]
        wgb = A.alloc([16, 24], BF16)
        ccs = A.alloc([512], F32)
        ubuf = A.alloc([4, 130], F32)
        ybuf = A.alloc([4, 128], F32)
        ybuf2 = A.alloc([4, 128], F32)
        hcs = A.alloc([16], F32)
        t_hTo = P.tok("hTo")
        t_wbuf = P.toks(2, "wbuf")
        t_misc = P.tok("miscB")
        P.add("sp", lambda e: e.dma_start(out=convw, in_=convw_d), writes=[t_misc], dma=True)
        P.add("pool", lambda e: e.dma_start(out=wgb, in_=wg_d.rearrange("(k p) n -> p k n", p=128)), writes=[t_misc], dma=True)
        units = [("q", u) for u in range(4)] + [("c", c) for c in range(8)]

        def load_unit(ui):
            kind, idx = units[ui]
            wb = wbuf[ui % 2]
            if kind == "q":
                for q2 in range(2):
                    P.add("pool", lambda e, wb=wb, idx=idx, q2=q2: e.dma_start(
                        out=wb[:, q2 * 8:(q2 + 1) * 8, 0:256],
                        in_=wq_d[q2 * 1024:(q2 + 1) * 1024, idx * 256:(idx + 1) * 256].rearrange("(k p) n -> p k n", p=128)),
                        writes=[t_wbuf[ui % 2]], dma=True)
            else:
                for q2 in range(2):
                    P.add("pool", lambda e, wb=wb, idx=idx, q2=q2: e.dma_start(
                        out=wb[:, q2 * 8:(q2 + 1) * 8, 0:384],
                        in_=wcv_d[q2 * 1024:(q2 + 1) * 1024, idx, :].rearrange("(k p) n -> p k n", p=128)),
                        writes=[t_wbuf[ui % 2]], dma=True)
        load_unit(0)
        load_unit(1)
        pipeline(NOWN, [
            lambda i: norm_s0(("b", i), xo[i * 128:(i + 1) * 128, :], i % 2),
            lambda i: norm_s1(("b", i), gA, i % 2),
            lambda i: transpose_block(i % 2, 2 * (i % 2), hTo[:, :, i * 128:(i + 1) * 128], t_hTo, "act" if i % 2 == 0 else "dve"),
        ])
        norm_block(xh, gA, 0, n_rows=16)
        t_hTh = P.tok("hTh")
        transpose_block(0, 0, hTh, t_hTh, "act", n_rows=16)
        t_gates = P.tok("gates")
        for i in range(NOWN):
            pb = 4 + i % 2

            def mmg(e, i=i, pb=pb):
                for k in range(16):
                    ins = e.matmul(bank(pb, 24), lhsT=hTo[:, k, i * 128:(i + 1) * 128], rhs=wgb[:, k, :],
                                   start=(k == 0), stop=(k == 15))
                return ins
            P.add("pe", mmg, reads=[t_hTo, t_misc], writes=[tP[pb]])
            P.add("act", lambda e, i=i, pb=pb: e.activation(out=gates[:, i, :], in_=bank(pb, 24), func=AF.Sigmoid),
                  reads=[tP[pb]], writes=[t_gates])
        t_ccs, t_u, t_y, t_hcs, t_y2 = P.toks(5, "cv")
        pbi = 0
        cbi = [0]
        for ui, (kind, idx) in enumerate(units):
            wb = wbuf[ui % 2]
            t_wb = t_wbuf[ui % 2]
            if kind == "q":
                for hl in range(2):
                    for half in range(2):
                        pb = 4 + pbi % 4
                        pbi += 1

                        def mmq(e, wb=wb, hl=hl, half=half, pb=pb):
                            for k in range(16):
                                ins = e.matmul(bank(pb), lhsT=wb[:, k, hl * 128:(hl + 1) * 128],
                                               rhs=hTo[:, k, half * 512:(half + 1) * 512], start=(k == 0), stop=(k == 15))
                            return ins
                        P.add("pe", mmq, reads=[t_hTo, t_wb], writes=[tP[pb]])
                        dv = qT[:, idx * 2 + hl, half * 512:(half + 1) * 512]
                        if pbi % 2 == 0:
                            P.add("act", lambda e, dv=dv, pb=pb: e.copy(out=dv, in_=bank(pb)), reads=[tP[pb]], writes=[tP[pb]])
                        else:
                            P.add("dve", lambda e, dv=dv, pb=pb: e.tensor_copy(out=dv, in_=bank(pb)), reads=[tP[pb]], writes=[tP[pb]])
            else:
                cch = idx
                def mmh(e, wb=wb):
                    for s, ty in enumerate((0, 2)):
                        for k in range(16):
                            ins = e.matmul(bank(3)[:, s * 16:(s + 1) * 16], lhsT=wb[:, k, ty * 128:(ty + 1) * 128], rhs=hTh[:, k, :],
                                           start=(k == 0), stop=(k == 15))
                    return ins
                P.add("pe", mmh, reads=[t_hTh, t_wb], writes=[tP[3]])
                P.add("act", lambda e: e.copy(out=hcs, in_=bank(3)[:, 16:32]), reads=[tP[3]], writes=[t_hcs])
                for half in range(2):
                    pbs = []
                    for ty in range(3):
                        pb = (0, 1, 2, 4, 5, 6)[cbi[0] % 6]
                        cbi[0] += 1
                        pbs.append(pb)

                        def mmc2(e, wb=wb, ty=ty, half=half, pb=pb):
                            for k in range(16):
                                ins = e.matmul(bank(pb), lhsT=wb[:, k, ty * 128:(ty + 1) * 128],
                                               rhs=hTo[:, k, half * 512:(half + 1) * 512], start=(k == 0), stop=(k == 15))
                            return ins
                        P.add("pe", mmc2, reads=[t_hTo, t_wb], writes=[tP[pb]])
                    p_ch, p_cb, p_cc = pbs
                    P.add("act", lambda e, p_cc=p_cc: e.copy(out=ccs, in_=bank(p_cc)), reads=[tP[p_cc]], writes=[t_ccs, tP[p_cc]])
                    P.add("dve", lambda e, p_ch=p_ch: e.tensor_tensor(
                        out=ubuf[:, :, 2:130], in0=bank(p_ch).rearrange("p (a b) -> p a b", a=4),
                        in1=ccs.rearrange("p (a b) -> p a b", a=4), op=ALU.mult),
                        reads=[tP[p_ch], t_ccs], writes=[t_u, tP[p_ch]])
                    P.add("dve", lambda e, half=half: e.tensor_tensor(
                        out=ubuf[:, :, 0:2], in0=bank(3)[:, half * 8:(half + 1) * 8].rearrange("p (a b) -> p a b", a=4),
                        in1=hcs[:, half * 8:(half + 1) * 8].rearrange("p (a b) -> p a b", a=4), op=ALU.mult),
                        reads=[tP[3], t_hcs], writes=[t_u])
                    P.add("dve", lambda e, cch=cch: e.tensor_scalar(out=ybuf, in0=ubuf[:, :, 0:128], scalar1=convw[:, cch, 0:1],
                                                                   scalar2=None, op0=ALU.mult),
                          reads=[t_u, t_misc], writes=[t_y])
                    for kk in (1, 2):
                        P.add("dve", lambda e, cch=cch, kk=kk: e.scalar_tensor_tensor(
                            out=ybuf, in0=ubuf[:, :, kk:kk + 128], scalar=convw[:, cch, kk:kk + 1], in1=ybuf,
                            op0=ALU.mult, op1=ALU.add), reads=[t_u, t_y], writes=[t_y])
                    P.add("dve", lambda e, cch=cch, half=half, p_cb=p_cb: e.tensor_tensor(
                        out=ocT[:, cch, half * 512:(half + 1) * 512].rearrange("p (a b) -> p a b", a=4),
                        in0=ybuf, in1=bank(p_cb).rearrange("p (a b) -> p a b", a=4), op=ALU.mult),
                        reads=[t_y, tP[p_cb]], writes=[tP[p_cb]])
            if ui + 2 < len(units):
                load_unit(ui + 2)
        dbg_dump("qT", qT, [128, 8, 1024], BF16)
        dbg_dump("ocT", ocT, [128, 8, 1024], BF16)
        dbg_dump("gates", gates, [128, NOWN, 24])
        P.barrier()

        if stop_after == "B":
            P.cut = True
        A.seek(OFF_XN, 35072)
        TW = A.alloc([2, 2, 4, 128], F32)
        A.seek(OFF_XIN, OFF_XN)
        TC = A.alloc([2, 4, 4, 128], F32)
        A.seek(159744, 178176)
        wob0 = A.alloc([16, 512], BF16)
        t_wob = P.toks(2, "wob")
        t_oaoc = P.tok("oaoc")
        A.seek(OFF_BC, 159744)
        mkd = A.alloc([128], F32)
        mkf = A.alloc([128], F32)
        mkc = A.alloc([4, 128], F32)
        b31 = A.alloc([8], F32)
        impmax = A.alloc([NOWN, 64], F32)
        impmin = A.alloc([NOWN, 64], F32)
        Ebuf = [A.alloc([512], F32) for _ in range(2)]
        pbuf = [A.alloc([4, 128], BF16) for _ in range(4)]
        Eoh = A.alloc([NB, 128], BF16)
        nselT = [A.alloc([128], BF16) for _ in range(2)]
        selb = A.alloc([64], BF16)
        impb = [A.alloc([64], F32) for _ in range(2)]
        m8 = A.alloc([16], F32)
        rden = [A.alloc([3, 4], F32) for _ in range(2)]
        cf = [A.alloc([3, 4], F32) for _ in range(2)]
        dtmp = A.alloc([4], F32)
        oacc = [A.alloc([4, 128], F32) for _ in range(2)]
        oab = A.alloc([4, 128], BF16)
        t_tab = P.tok("tab")
        for (dst, srcd) in ((TW, bw_d.rearrange("k g a h t -> k (g a h t)")), (TC, bc_d.rearrange("k g a h t -> k (g a h t)")),
                            (mkd, mkd_d), (mkf, mkf_d), (mkc, mkc_d.rearrange("k a t -> k (a t)")), (b31, b31_d),
                            (impmax, impmax_d.rearrange("k a t -> k (a t)")), (impmin, impmin_d.rearrange("k a t -> k (a t)"))):
            dflat = dst
            nd = len(dst.shape)
            if nd == 3:
                dflat = dst.rearrange("p a b -> p (a b)")
            elif nd == 5:
                dflat = dst.rearrange("p a b c d -> p (a b c d)")
            P.add("sp", lambda e, dflat=dflat, srcd=srcd: e.dma_start(out=dflat, in_=srcd), writes=[t_tab], dma=True)
        P.add("pool", lambda e: e.dma_start(out=Eoh[0:64].rearrange("p a b -> p (a b)"), in_=eoh_d), writes=[t_tab], dma=True)
        P.add("dve", lambda e: e.memset(Eoh[64:128], 0.0), writes=[t_tab])
        for pp_ in range(2):
            P.add("dve", lambda e, pp_=pp_: e.memset(nselT[pp_][64:128], 0.0), writes=[t_tab])
        for g in range(2):
            for h in range(4):
                bcol = b31[:, g * 4 + h:g * 4 + h + 1]
                P.add("dve", lambda e, g=g, h=h, bcol=bcol: e.scalar_tensor_tensor(
                    out=TW[:, g, 0, h, :], in0=TW[:, g, 0, h, :], scalar=bcol, in1=mkd, op0=ALU.subtract, op1=ALU.add),
                    reads=[t_tab], writes=[t_tab])
                P.add("dve", lambda e, g=g, h=h, bcol=bcol: e.tensor_scalar(
                    out=TW[:, g, 1, h, :], in0=TW[:, g, 1, h, :], scalar1=bcol, scalar2=None, op0=ALU.subtract),
                    reads=[t_tab], writes=[t_tab])
                for off in range(4):
                    P.add("dve", lambda e, g=g, h=h, off=off, bcol=bcol: e.scalar_tensor_tensor(
                        out=TC[:, g, off, h, :], in0=TC[:, g, off, h, :], scalar=bcol, in1=mkc[:, off, :],
                        op0=ALU.subtract, op1=ALU.add), reads=[t_tab], writes=[t_tab])
        dbg_dump("TW", TW, [128, 2, 2, 4, 128])
        dbg_dump("TC", TC, [128, 2, 4, 4, 128])
        P.barrier()
        for q4 in range(4):
            P.add("pool", lambda e, q4=q4: e.dma_start(
                out=wob0[:, q4 * 4:(q4 + 1) * 4, :],
                in_=wo_d[q4 * 512:(q4 + 1) * 512, 0:512].rearrange("(k p) n -> p k n", p=128)),
                writes=[t_wob[0]], dma=True)

        NIG = 2 * NOWN
        SB = (0, 1, 6)
        t_E = P.toks(2, "E")
        t_p = P.toks(4, "p")
        t_selb, t_m8, t_oab = P.toks(3, "at")
        t_nsel = P.toks(2, "nsel")
        t_imp = P.toks(2, "imp")
        t_rden = P.toks(2, "rden")
        t_cf = P.toks(2, "cf")
        t_oacc = P.toks(2, "oacc")
        t_dtmp = P.tok("dtmp")
        ecnt = [0]

        def acc_view(b0, w):
            return [bank(b0 + h // 2)[:, (h % 2) * w:(h % 2) * w + w] for h in range(4)]

        deferred = []

        def flush(key):
            for d in [d for d in deferred if d[1] == key]:
                d[2]()
                deferred.remove(d)

        def qslice(n):
            i, g = n // 2, n % 2
            return qT[:, 4 * g:4 * g + 4, i * 128:(i + 1) * 128]

        def post_C(n, step):
            i, g = n // 2, n % 2
            pp = n % 2
            accC = acc_view(2, 193)
            for bb in range(2):
                P.add("dve", lambda e, bb=bb: e.tensor_scalar_max(
                    out=dtmp[:, 2 * bb:2 * bb + 2],
                    in0=bank(2 + bb)[:, 0:386].rearrange("p (a w) -> p a w", a=2)[:, :, 128], scalar1=1e-30),
                    reads=[tP[2 + bb]], writes=[t_dtmp])
                P.add("dve", lambda e, bb=bb: e.reciprocal(out=rden[pp][:, 0, 2 * bb:2 * bb + 2], in_=dtmp[:, 2 * bb:2 * bb + 2]),
                      reads=[t_dtmp], writes=[t_rden[pp]])
            gv = gates[:, i, g * 12:(g + 1) * 12].rearrange("p (h b) -> p h b", b=3)
            P.add("dve", lambda e: e.tensor_tensor(out=cf[pp][:, 0, :], in0=gv[:, :, 0], in1=rden[pp][:, 0, :], op=ALU.mult),
                  reads=[t_rden[pp]], writes=[t_cf[pp]])

            def evac_c():
                for h in range(4):
                    P.add("act", lambda e, h=h: e.activation(out=oacc[pp][:, h, :], in_=accC[h][:, 0:128], func=AF.Copy,
                                                             scale=cf[pp][:, 0, h:h + 1]),
                          reads=[tP[2 + h // 2], t_cf[pp]], writes=[t_oacc[pp]])
            deferred.append((step + 2, ("c", n), evac_c))
            ib = impb[0]
            P.add("dve", lambda e: e.tensor_scalar(out=ib, in0=accC[0][:, 129:193], scalar1=rden[pp][:, 0, 0:1],
                                                   scalar2=None, op0=ALU.mult),
                  reads=[tP[2], t_rden[pp]], writes=[t_imp[0]])
            for h in range(1, 4):
                P.add("dve", lambda e, h=h: e.scalar_tensor_tensor(
                    out=ib, in0=accC[h][:, 129:193], scalar=rden[pp][:, 0, h:h + 1], in1=ib, op0=ALU.mult, op1=ALU.add),
                    reads=[tP[2 + h // 2], t_rden[pp], t_imp[0]], writes=[t_imp[0]])
            P.add("dve", lambda e: e.tensor_tensor(out=ib, in0=ib, in1=impmax[:, i, :], op=ALU.max),
                  reads=[t_imp[0], t_tab], writes=[t_imp[0]])
            P.add("dve", lambda e: e.tensor_tensor(out=ib, in0=ib, in1=impmin[:, i, :], op=ALU.min),
                  reads=[t_imp[0]], writes=[t_imp[0]])
            P.add("dve", lambda e: e.max(out=m8[:, 0:8], in_=ib), reads=[t_imp[0]], writes=[t_m8])
            P.add("dve", lambda e: e.match_replace(out=impb[1], in_to_replace=m8[:, 0:8], in_values=ib, imm_value=-3.0e38),
                  reads=[t_imp[0], t_m8], writes=[t_imp[1]])
            P.add("dve", lambda e: e.max(out=m8[:, 8:16], in_=impb[1]), reads=[t_imp[1]], writes=[t_m8])
            P.add("dve", lambda e: e.tensor_tensor(out=selb, in0=ib, in1=m8[:, 15:16].to_broadcast([128, 64]), op=ALU.is_ge),
                  reads=[t_imp[0], t_m8], writes=[t_selb])

            def tr_sel():
                tv = bankb(7, 128)[0:64]

                def trs(e):
                    return e.transpose(out=tv, in_=selb, identity=ident)
                P.add("pe", trs, reads=[t_selb], writes=[tP[7]])
                P.add("dve", lambda e: e.tensor_scalar(out=nselT[pp][0:64], in0=tv, scalar1=-1.0, scalar2=-NEGM / SCALE,
                                                       op0=ALU.add, op1=ALU.mult),
                      reads=[tP[7]], writes=[t_nsel[pp]])
            deferred.append((step + 5, ("sel", n), tr_sel))

        def post_W(n):
            i, g = n // 2, n % 2
            pp = n % 2
            flush(("c", n))
            accW = acc_view(4, 129)
            gv = gates[:, i, g * 12:(g + 1) * 12].rearrange("p (h b) -> p h b", b=3)
            for bb in range(2):
                P.add("dve", lambda e, bb=bb: e.reciprocal(
                    out=rden[pp][:, 1, 2 * bb:2 * bb + 2],
                    in_=bank(4 + bb)[:, 0:258].rearrange("p (a w) -> p a w", a=2)[:, :, 128]),
                    reads=[tP[4 + bb]], writes=[t_rden[pp]])
            P.add("dve", lambda e: e.tensor_tensor(out=cf[pp][:, 1, :], in0=gv[:, :, 2], in1=rden[pp][:, 1, :], op=ALU.mult),
                  reads=[t_rden[pp]], writes=[t_cf[pp]])
            for h in range(4):
                P.add("dve", lambda e, h=h: e.scalar_tensor_tensor(
                    out=oacc[pp][:, h, :], in0=accW[h][:, 0:128], scalar=cf[pp][:, 1, h:h + 1], in1=oacc[pp][:, h, :],
                    op0=ALU.mult, op1=ALU.add),
                    reads=[tP[4 + h // 2], t_cf[pp], t_oacc[pp]], writes=[t_oacc[pp]])

        def post_S(n, step):
            i, g = n // 2, n % 2
            pp = n % 2
            accS = acc_view(2, 129)
            gv = gates[:, i, g * 12:(g + 1) * 12].rearrange("p (h b) -> p h b", b=3)
            for bb in range(2):
                P.add("dve", lambda e, bb=bb: e.reciprocal(
                    out=rden[pp][:, 2, 2 * bb:2 * bb + 2],
                    in_=bank(2 + bb)[:, 0:258].rearrange("p (a w) -> p a w", a=2)[:, :, 128]),
                    reads=[tP[2 + bb]], writes=[t_rden[pp]])
            P.add("dve", lambda e: e.tensor_tensor(out=cf[pp][:, 2, :], in0=gv[:, :, 1], in1=rden[pp][:, 2, :], op=ALU.mult),
                  reads=[t_rden[pp]], writes=[t_cf[pp]])
            for h in range(4):
                P.add("dve", lambda e, h=h: e.scalar_tensor_tensor(
                    out=oab[:, h, :], in0=accS[h][:, 0:128], scalar=cf[pp][:, 2, h:h + 1], in1=oacc[pp][:, h, :],
                    op0=ALU.mult, op1=ALU.add),
                    reads=[tP[2 + h // 2], t_cf[pp], t_oacc[pp]], writes=[t_oab] + ([t_oacc[pp]] if h == 3 else []))

            def tr_out():
                tv = bankb(7, 512).rearrange("p (h t) -> p h t", h=4)

                def tro(e):
                    for h in range(4):
                        ins = e.transpose(out=tv[:, h, :], in_=oab[:, h, :], identity=ident)
                    return ins
                P.add("pe", tro, reads=[t_oab], writes=[tP[7]])
                P.add("act", lambda e: e.copy(out=oaT[:, 4 * g:4 * g + 4, i * 128:(i + 1) * 128], in_=tv),
                      reads=[tP[7]], writes=[t_oab])
            deferred.append((step + 3, ("o", n), tr_out))

        tiles = []

        def add_C(n):
            i, g = n // 2, n % 2
            nch = 1 if i < 4 else 2
            off = i % 4
            nr = 32 * off + 32
            for ch in range(nch):
                last = ch == nch - 1
                nrows = 128
                tiles.append(dict(kind="C", n=n, nrows=nrows, lhsT=kcmpT[:, g, ch * 128:ch * 128 + nrows],
                                  table=(TC[:nrows, g, off] if last else None), first=(ch == 0), last=last,
                                  rhs=RC[:nrows, ch, g, :], acc=acc_view(2, 193), accb=(2, 3)))

        def add_W(n):
            i, g = n // 2, n % 2
            qb = 4 * i + 3
            kbs = [kb for kb in range(qb - 4, qb + 1) if kb >= 0]
            for kb in kbs:
                if kb == qb:
                    table = TW[:, g, 0]
                elif kb == qb - 1:
                    table = TW[:, g, 1]
                elif kb == qb - 4:
                    table = mkf.unsqueeze(1).to_broadcast([128, 4, 128])
                else:
                    table = None
                tiles.append(dict(kind="W", n=n, nrows=128, lhsT=kwT[:, g, kb * 128:(kb + 1) * 128], table=table,
                                  first=(kb == kbs[0]), last=(kb == qb), rhs=vwA[:, kb, g, :], acc=acc_view(4, 129), accb=(4, 5)))

        def add_S(n):
            i, g = n // 2, n % 2
            qb = 4 * i + 3
            for kb in range(qb + 1):
                if kb == qb:
                    table = TW[:, g, 0]
                elif kb == qb - 1:
                    table = TW[:, g, 1]
                else:
                    table = None
                tiles.append(dict(kind="S", n=n, nrows=128, lhsT=ksT[:, g, kb * 128:(kb + 1) * 128], table=table,
                                  first=(kb == 0), last=(kb == qb), rhs=vsA[:, kb, g, :], acc=acc_view(2, 129), accb=(2, 3), kb=kb))
        add_C(0)
        add_W(0)
        for n in range(NIG - 1):
            add_C(n + 1)
            add_W(n + 1)
            add_S(n)
        add_S(NIG - 1)
        for idx, t in enumerate(tiles):
            t["idx"] = idx

        def c_s0(idx):
            t = tiles[idx]
            sb_ = SB[idx % 3]
            nrows = t["nrows"]
            sps = bank(sb_).rearrange("p (h t) -> p h t", h=4)
            qsl = qslice(t["n"])
            if t["kind"] == "S":
                pp = t["n"] % 2
                kb = t["kb"]
                if t["first"]:
                    flush(("sel", t["n"]))

                def mms(e):
                    e.matmul(sps, lhsT=t["lhsT"], rhs=qsl, start=True, stop=False)
                    return e.matmul(sps, lhsT=Eoh[:, kb, :], rhs=nselT[pp].unsqueeze(1).to_broadcast([128, 4, 128]),
                                    start=False, stop=True)
                P.add("pe", mms, reads=[t_nsel[pp]], writes=[tP[sb_]])
            else:
                P.add("pe", lambda e: e.matmul(sps[:nrows], lhsT=t["lhsT"], rhs=qsl, start=True, stop=True),
                      reads=[], writes=[tP[sb_]])

        def c_s1(idx):
            t = tiles[idx]
            sb_ = SB[idx % 3]
            nrows = t["nrows"]
            sps = bank(sb_).rearrange("p (h t) -> p h t", h=4)
            dst, t_dst = pbuf[idx % 4], t_p[idx % 4]
            if t["table"] is None:
                P.add("act", lambda e: e.activation(out=dst[:nrows], in_=sps[:nrows], func=AF.Exp, scale=SCALE),
                      reads=[tP[sb_]], writes=[t_dst])
            else:
                ei = ecnt[0] % 2
                ecnt[0] += 1
                eb = Ebuf[ei].rearrange("p (h t) -> p h t", h=4)
                P.add("dve", lambda e: e.scalar_tensor_tensor(out=eb[:nrows], in0=sps[:nrows], scalar=SCALE, in1=t["table"],
                                                              op0=ALU.mult, op1=ALU.add),
                      reads=[tP[sb_], t_tab], writes=[t_E[ei]])
                P.add("act", lambda e: e.activation(out=dst[:nrows], in_=eb[:nrows], func=AF.Exp),
                      reads=[t_E[ei]], writes=[t_dst])

        def c_s3(idx, step):
            t = tiles[idx]
            nrows = t["nrows"]
            src, t_src = pbuf[idx % 4], t_p[idx % 4]
            acc = t["acc"]

            def pv(e):
                for h in range(4):
                    ins = e.matmul(acc[h], lhsT=src[:nrows, h, :], rhs=t["rhs"],
                                   start=(t["first"] and h % 2 == 0), stop=(t["last"] and h % 2 == 1))
                return ins
            P.add("pe", pv, reads=[t_src], writes=[tP[t["accb"][0]], tP[t["accb"][1]]])
            if t["last"]:
                if t["kind"] == "C":
                    post_C(t["n"], step)
                elif t["kind"] == "W":
                    post_W(t["n"])
                else:
                    post_S(t["n"], step)

        NT = len(tiles)
        for step in range(NT + 3):
            for d in [d for d in deferred if d[0] <= step]:
                d[2]()
                deferred.remove(d)
            if 0 <= step - 3 < NT:
                c_s3(step - 3, step)
            if 0 <= step - 1 < NT:
                c_s1(step - 1)
            if step < NT:
                c_s0(step)
        for d in list(deferred):
            d[2]()
        deferred[:] = []
        dbg_dump("oaT", oaT, [128, 8, 1024], BF16)
        P.barrier()

        if stop_after == "C":
            P.cut = True
        A.seek(35072, 159744)
        h2T = A.alloc([16, 1024], BF16)
        mix = A.alloc([NOWN, D], F32)
        gB = A.alloc([D], F32)
        wob = [wob0, A.alloc([16, 512], BF16)]
        t_mix = P.toks(NOWN, "mix")
        t_g = P.tok("g")
        P.add("sp", lambda e: e.dma_start(out=gA, in_=gains_d[1]), writes=[t_g], dma=True)
        P.add("sp", lambda e: e.dma_start(out=gB, in_=gains_d[2]), writes=[t_g], dma=True)

        def load_wo(n):
            for q4 in range(4):
                P.add("pool", lambda e, n=n, q4=q4: e.dma_start(
                    out=wob[n % 2][:, q4 * 4:(q4 + 1) * 4, :],
                    in_=wo_d[q4 * 512:(q4 + 1) * 512, n * 512:(n + 1) * 512].rearrange("(k p) n -> p k n", p=128)),
                    writes=[t_wob[n % 2]], dma=True)
        load_wo(1)
        pbi_ = [0]

        def wo_group(n, blk):
            pb = 4 + pbi_[0] % 4
            pbi_[0] += 1

            def mmo(e):
                for k in range(16):
                    lt = oaT[:, k, blk * 128:(blk + 1) * 128] if k < 8 else ocT[:, k - 8, blk * 128:(blk + 1) * 128]
                    ins = e.matmul(bank(pb), lhsT=lt, rhs=wob[n % 2][:, k, :], start=(k == 0), stop=(k == 15))
                return ins
            P.add("pe", mmo, reads=[t_wob[n % 2], t_oaoc], writes=[tP[pb]])
            dv = mix[:, blk, n * 512:(n + 1) * 512]
            if pbi_[0] % 2 == 0:
                P.add("act", lambda e: e.copy(out=dv, in_=bank(pb)), reads=[tP[pb]], writes=[t_mix[blk], tP[pb]])
            else:
                P.add("dve", lambda e: e.tensor_copy(out=dv, in_=bank(pb)), reads=[tP[pb]], writes=[t_mix[blk], tP[pb]])
        for n in range(3):
            for blk in range(NOWN):
                wo_group(n, blk)
            if n + 2 < 4:
                load_wo(n + 2)
        A.seek(67840)
        facc = A.alloc([NOWN, D], F32)
        wub1 = A.alloc([16, 512], BF16)
        wdb1 = A.alloc([4, D], BF16)
        junkF = A.alloc([D], BF16)
        A.seek(178176)
        wub = [A.alloc([16, 512], BF16), wub1]
        wdb = [A.alloc([4, D], BF16), wdb1]
        A.seek(OFF_XIN, OFF_XN)
        rbuf = [A.alloc([512], F32) for _ in range(2)]
        aT0 = A.alloc([4, 1024], BF16)
        A.seek(OFF_XN, 35072)
        aT = [aT0, A.alloc([4, 1024], BF16)]
        t_wub = P.toks(2, "wub")
        t_wdb = P.toks(2, "wdb")
        t_aT = P.toks(2, "aT")
        t_rb = P.toks(2, "rb")
        t_facc = P.toks(NOWN, "facc")
        NG = DFF // 512

        def load_wu(G, extra=()):
            for q4 in range(4):
                P.add("pool", lambda e, G=G, q4=q4: e.dma_start(
                    out=wub[G % 2][:, q4 * 4:(q4 + 1) * 4, :],
                    in_=wup_d[q4 * 512:(q4 + 1) * 512, G * 512:(G + 1) * 512].rearrange("(k p) n -> p k n", p=128)),
                    writes=[t_wub[G % 2]] + list(extra), dma=True)

        def load_wd(G, extra=()):
            for q4 in range(4):
                P.add("pool", lambda e, G=G, q4=q4: e.dma_start(
                    out=wdb[G % 2][:, q4:q4 + 1, :],
                    in_=wdn_d[G * 512 + q4 * 128:G * 512 + (q4 + 1) * 128, :].rearrange("(k p) n -> p k n", p=128)),
                    writes=[t_wdb[G % 2]] + list(extra), dma=True)
        t_h2T = P.tok("h2T")
        t_x1s = P.toks(NOWN, "x1s")
        junkD = wob[0][:, 0:4, :].rearrange("p a b -> p (a b)")
        t_junkD = t_wob[0]
        dstate = {}

        def d_s0(blk):
            xi = blk % 2
            P.add("sp", lambda e: e.dma_start(out=xin[xi], in_=xo[blk * 128:(blk + 1) * 128, :]),
                  writes=[t_xin[xi]], dma=True)

        def d_s1(blk):
            xi = blk % 2
            mb = mix[:, blk, :]
            c, t_c = dstate.pop(blk)
            rms_fin(c, t_c)
            P.add("dve", lambda e: e.scalar_tensor_tensor(out=mb, in0=mb, scalar=c[:, 3:4], in1=gA,
                                                          op0=ALU.mult, op1=ALU.mult),
                  reads=[t_c, t_g], writes=[t_mix[blk]])
            P.add("dve", lambda e: e.tensor_tensor(out=mb, in0=mb, in1=xin[xi], op=ALU.add),
                  reads=[t_xin[xi]], writes=[t_mix[blk]])
            P.add("pool", lambda e: e.dma_start(out=x1s[blk * 128:(blk + 1) * 128, :], in_=mb),
                  reads=[t_mix[blk]], writes=[t_x1s[blk]], dma=True)
            dstate[("b", blk)] = rms_sq(mb, t_mix[blk], junkD, t_junkD)

        def d_s2(blk):
            xi = blk % 2
            mb = mix[:, blk, :]
            c2, t_c2 = dstate.pop(("b", blk))
            rms_fin(c2, t_c2)
            P.add("dve", lambda e: e.scalar_tensor_tensor(out=xn[xi], in0=mb, scalar=c2[:, 3:4], in1=gB,
                                                          op0=ALU.mult, op1=ALU.mult),
                  reads=[t_c2, t_mix[blk], t_g], writes=[t_xn[xi]])

        def d_s3(blk):
            xi = blk % 2
            transpose_block(xi, 2 * xi, h2T[:, :, blk * 128:(blk + 1) * 128], t_h2T, "act")
        d_stages = [d_s0, d_s1, d_s2, d_s3]
        for step in range(NOWN + 5):
            if step < NOWN:
                wo_group(3, step)
                dstate[step] = rms_sq(mix[:, step, :], t_mix[step], junkD, t_junkD)
            if step == NOWN:
                load_wu(0, extra=[t_oaoc])
                load_wd(0, extra=[t_oaoc])
            for k in (3, 2, 1, 0):
                it = step - 1 - k
                if 0 <= it < NOWN:
                    d_stages[k](it)
        P.barrier()

        if stop_after == "D":
            P.cut = True
        ucnt = [0]
        dcnt = [0]

        def up(G):
            for fc in range(4):
                for half in range(2):
                    pb = ucnt[0] % 4
                    ri = ucnt[0] % 2
                    ucnt[0] += 1

                    def mmu(e, G=G, fc=fc, half=half, pb=pb):
                        for k in range(16):
                            ins = e.matmul(bank(pb), lhsT=wub[G % 2][:, k, fc * 128:(fc + 1) * 128],
                                           rhs=h2T[:, k, half * 512:(half + 1) * 512], start=(k == 0), stop=(k == 15))
                        return ins
                    P.add("pe", mmu, reads=[t_wub[G % 2]], writes=[tP[pb]])
                    P.add("act", lambda e, pb=pb, ri=ri: e.activation(out=rbuf[ri], in_=bank(pb), func=AF.Relu),
                          reads=[tP[pb]], writes=[t_rb[ri], tP[pb]])
                    P.add("pool", lambda e, G=G, fc=fc, half=half, ri=ri: e.tensor_tensor(
                        out=aT[G % 2][:, fc, half * 512:(half + 1) * 512], in0=rbuf[ri], in1=rbuf[ri], op=ALU.mult),
                        reads=[t_rb[ri]], writes=[t_aT[G % 2]])

        def down(G):
            for blk in range(NOWN):
                for n in range(4):
                    pb = 4 + dcnt[0] % 4
                    dcnt[0] += 1

                    def mmd(e, G=G, blk=blk, n=n, pb=pb):
                        for fc in range(4):
                            ins = e.matmul(bank(pb), lhsT=aT[G % 2][:, fc, blk * 128:(blk + 1) * 128],
                                           rhs=wdb[G % 2][:, fc, n * 512:(n + 1) * 512], start=(fc == 0), stop=(fc == 3))
                        return ins
                    P.add("pe", mmd, reads=[t_aT[G % 2], t_wdb[G % 2]], writes=[tP[pb]])
                    fv = facc[:, blk, n * 512:(n + 1) * 512]
                    if G == 0:
                        P.add("act", lambda e, fv=fv, pb=pb: e.copy(out=fv, in_=bank(pb)), reads=[tP[pb]],
                              writes=[t_facc[blk], tP[pb]])
                    else:
                        P.add("dve", lambda e, fv=fv, pb=pb: e.tensor_tensor(out=fv, in0=bank(pb), in1=fv, op=ALU.add),
                              reads=[tP[pb]], writes=[t_facc[blk], tP[pb]])
                if G == NG - 1:
                    if blk >= 1:
                        f_s1(blk - 1)
                    f_s0(blk)
            if G == NG - 1:
                f_s1(NOWN - 1)
        t_g2 = P.tok("g2")
        P.add("sp", lambda e: e.dma_start(out=gA, in_=gains_d[3]), writes=[t_g2], dma=True)
        t_out = P.tok("out")
        t_junkF = P.tok("junkF")
        xinF = [wub[i][:, 0:8, :].rearrange("p a b -> p (a b)").bitcast(F32) for i in range(2)]
        fstate = {}

        def f_s0(blk):
            xi = blk % 2
            fb = facc[:, blk, :]
            P.add("sp", lambda e: e.dma_start(out=xinF[xi], in_=x1s[blk * 128:(blk + 1) * 128, :]),
                  reads=[t_x1s[blk]], writes=[t_wub[xi]], dma=True)
            fstate[blk] = rms_sq(fb, t_facc[blk], junkF, t_junkF)

        def f_s1(blk):
            xi = blk % 2
            fb = facc[:, blk, :]
            c, t_c = fstate.pop(blk)
            rms_fin(c, t_c)
            P.add("dve", lambda e: e.scalar_tensor_tensor(out=fb, in0=fb, scalar=c[:, 3:4], in1=gA,
                                                          op0=ALU.mult, op1=ALU.mult),
                  reads=[t_c, t_g2], writes=[t_facc[blk]])
            P.add("dve", lambda e: e.tensor_tensor(out=fb, in0=fb, in1=xinF[xi], op=ALU.add),
                  reads=[t_wub[xi]], writes=[t_facc[blk]])
            P.add("pool", lambda e: e.dma_start(out=out_d[blk * 128:(blk + 1) * 128, :], in_=fb),
                  reads=[t_facc[blk]], writes=[t_out], dma=True)

        load_wu(1)
        load_wd(1)
        up(0)
        for G in range(NG):
            if G + 1 < NG:
                up(G + 1)
            if G + 2 < NG:
                load_wu(G + 2)
            down(G)
            if G + 2 < NG:
                load_wd(G + 2)

        P.cut = False
        P.barrier()
        P.emit(block, csem, dsem)
    return nc, dbg_outs


def _rel_bucket(dist):
    n = np.maximum(dist, 0)
    nf = np.maximum(n, 1).astype(np.float32)
    large = 16 + (np.log(nf / np.float32(16)) / np.float32(math.log(128 / 16)) * np.float32(16)).astype(np.int32)
    large = np.minimum(large, 31)
    return np.where(n < 16, n, large)


def _host_inputs(inputs):
    x = np.ascontiguousarray(np.asarray(inputs["x"], dtype=np.float32))
    w_in = np.asarray(inputs["w_in"], dtype=np.float32)[0]
    rel_bias = np.asarray(inputs["rel_bias"], dtype=np.float32)
    wq = np.ascontiguousarray(w_in[:, 0:1024])
    o = 1024
    kc, vc, ks, vs, kw, vw = [w_in[:, o + 256 * n: o + 256 * (n + 1)] for n in range(6)]
    wkv = np.ascontiguousarray(np.concatenate([kc, vc, ks, kw, vs, vw], axis=1))
    wg = np.ascontiguousarray(w_in[:, 2560:2584])
    wc = w_in[:, 2584:5656].reshape(2048, 3, 8, 128).transpose(0, 2, 1, 3)
    wcv = np.ascontiguousarray(wc.reshape(2048, 8, 384))
    common = {
        "wq": wq, "wkv": wkv, "wg": wg, "wcv": wcv,
        "w1k": np.ascontiguousarray(inputs["w_cmp_k1"][0], dtype=np.float32),
        "w1v": np.ascontiguousarray(inputs["w_cmp_v1"][0], dtype=np.float32),
        "w2k": np.ascontiguousarray(inputs["w_cmp_k2"][0], dtype=np.float32),
        "w2v": np.ascontiguousarray(inputs["w_cmp_v2"][0], dtype=np.float32),
        "peT": np.ascontiguousarray(np.asarray(inputs["pe_cmp"], dtype=np.float32)[0].T),
        "convw": np.ascontiguousarray(np.asarray(inputs["conv_w"], dtype=np.float32)[0].reshape(3, 8, 128).transpose(2, 1, 0)),
        "wo": np.ascontiguousarray(inputs["w_o"][0], dtype=np.float32),
        "wup": np.ascontiguousarray(inputs["w_up"][0], dtype=np.float32),
        "wdn": np.ascontiguousarray(inputs["w_down"][0], dtype=np.float32),
        "ident": np.eye(128, dtype=np.float32),
    }
    gains = np.stack([np.asarray(inputs[k], dtype=np.float32)[0] for k in ("g_pre_mix", "g_post_mix", "g_pre_ffn", "g_post_ffn")])
    common["gains"] = np.ascontiguousarray(np.broadcast_to(gains[:, None, :], (4, 128, 2048)))
    kl = np.arange(128)[:, None]
    tl = np.arange(128)[None, :]
    bw = np.zeros((128, 2, 2, 4, 128), np.float32)
    for g in range(2):
        for tbl in range(2):
            bk = _rel_bucket(tl - kl + 128 * tbl)
            for h in range(4):
                bw[:, g, tbl, h, :] = rel_bias[4 * g + h][bk]
    common["bwraw"] = bw
    bc = np.zeros((128, 2, 4, 4, 128), np.float32)
    mkc = np.zeros((128, 4, 128), np.float32)
    r = np.arange(128)[:, None]
    for off in range(4):
        dist = 512 * off + 353 + tl - 16 * r
        bk = _rel_bucket(dist)
        mkc[:, off, :] = np.where(dist >= 0, 0.0, NEGM)
        for g in range(2):
            for h in range(4):
                bc[:, g, off, h, :] = rel_bias[4 * g + h][bk]
    common["bcraw"] = bc
    common["mkc"] = mkc
    common["b31"] = np.ascontiguousarray(np.broadcast_to(rel_bias[:, 31][None, :], (128, 8)))
    eoh = np.zeros((64, NB, 128), np.float32)
    for kb_ in range(NB):
        eoh[2 * kb_, kb_, 0:64] = 1.0
        eoh[2 * kb_ + 1, kb_, 64:128] = 1.0
    common["eoh"] = eoh.reshape(64, NB * 128)
    common["mkdiag"] = np.where(kl <= tl, 0.0, NEGM).astype(np.float32)
    common["mkfar"] = np.where(kl > tl, 0.0, NEGM).astype(np.float32)
    in_maps = []
    for c in range(8):
        b, j = c // 4, c % 4
        Pd = (3 - j) * 128
        m = dict(common)
        xbl = np.zeros((T, D), np.float32)
        xbl[Pd:] = x[b, :T - Pd]
        m["xb"] = xbl
        own = np.concatenate([np.arange((4 * i + j) * 128, (4 * i + j + 1) * 128) for i in range(NOWN)])
        m["xo"] = np.ascontiguousarray(x[b, own])
        xhl = np.zeros((16, D), np.float32)
        for i in range(NOWN):
            t0 = (4 * i + j) * 128
            if t0 >= 2:
                xhl[2 * i:2 * i + 2] = x[b, t0 - 2:t0]
        m["xh"] = xhl
        tloc = np.arange(128)[:, None] + 128 * np.arange(NB)[None, :]
        m["validk"] = (tloc >= Pd).astype(np.float32)
        cl = np.arange(128)[:, None] + 128 * np.arange(2)[None, :]
        validc = ((cl >= Pd // 16) & (cl <= 254)).astype(np.float32)
        jl = np.arange(64)[None, None, :]
        ci = (16 * cl)[:, :, None]
        ov = ((ci < 64 * jl + 64) & (ci + 32 > 64 * jl)).astype(np.float32)
        rcx = np.zeros((128, 2, 65), np.float32)
        rcx[:, :, 0] = validc
        rcx[:, :, 1:] = ov * validc[:, :, None]
        m["rcx"] = rcx
        imax = np.zeros((128, NOWN, 64), np.float32)
        imin = np.zeros((128, NOWN, 64), np.float32)
        first = Pd // 64
        for i in range(NOWN):
            t = (4 * i + 3) * 128 + np.arange(128)[:, None]
            cur = t // 64
            jj = np.arange(64)[None, :]
            mx = np.zeros((128, 64), np.float32)
            mx = np.where(jj == cur - 1, 1e9, mx)
            mx = np.where(jj == cur, 2e9, mx)
            mx = np.where(jj == first, 3e9, mx)
            imax[:, i, :] = mx
            imin[:, i, :] = np.where((jj * 64 <= t) & (jj >= first), 3.0e38, -1.0e30)
        m["impmax"] = imax
        m["impmin"] = imin
        in_maps.append(m)
    return in_maps


_CACHE = {}


def kernel(**inputs):
    in_maps = _host_inputs(inputs)
    if "nc" not in _CACHE:
        _CACHE["nc"] = build(False)[0]
    nc = _CACHE["nc"]
    res = run_bass_kernel_spmd(nc, in_maps, core_ids=list(range(8)))
    out = np.zeros((2, T, D), np.float32)
    for c in range(8):
        b, j = c // 4, c % 4
        o = np.asarray(res.results[c]["out"], dtype=np.float32)
        for i in range(NOWN):
            t0 = (4 * i + j) * 128
            out[b, t0:t0 + 128] = o[i * 128:(i + 1) * 128]
    return out
```
